# Optimizing a Trainium2 kernel written in Bass

```python
import math
import jax, jax.numpy as jnp
from jax import lax
import numpy as np

D_MODEL = 1024
BATCH = 8
SEQ = 4096
DEPTH = 1
DEC_BATCH = 128
DEC_SEQ = 8
PAST_LEN = 16384
PAGE_SIZE = 128

N_HEADS = 8
N_KV_HEADS = 2
HEAD_DIM = 64
Q_GROUP = N_HEADS // N_KV_HEADS
WINDOW = 128
ROPE_THETA = 10000.0
ATTN_WIDTH = N_HEADS * HEAD_DIM
KV_WIDTH = N_KV_HEADS * HEAD_DIM
SSM_WIDTH = 512
GROUP_SIZE = 16
N_GROUPS = SSM_WIDTH // GROUP_SIZE
STATE_DIM = 64
DT_MIN = 1e-3
DT_MAX = 1e-1
IN_WIDTH = ATTN_WIDTH + 2 * KV_WIDTH + SSM_WIDTH + 2 * D_MODEL
SPLIT_IDX = (ATTN_WIDTH, ATTN_WIDTH + KV_WIDTH, ATTN_WIDTH + 2 * KV_WIDTH,
             ATTN_WIDTH + 2 * KV_WIDTH + SSM_WIDTH, ATTN_WIDTH + 2 * KV_WIDTH + SSM_WIDTH + D_MODEL)
PEER_HEADS = 8
N_KEYS = 128
N_EXPERTS = N_KEYS * N_KEYS
KEY_DIM = 128
PEER_TOPK = 16
PEER_BLOCK = 128
PLE_DIM = 256
DN_ALPHA = (2.0 * DEPTH) ** 0.25
DN_BETA = (8.0 * DEPTH) ** -0.25
LN_EPS = 1e-5

kernel_name = 'swa_sink_s5_peer_hybrid_step'


def layer_norm(x, g, b):
    xf = x.astype(jnp.float32)
    mu = jnp.mean(xf, -1, keepdims=True)
    var = jnp.mean(jnp.square(xf - mu), -1, keepdims=True)
    return ((xf - mu) * lax.rsqrt(var + LN_EPS) * g.astype(jnp.float32) + b.astype(jnp.float32)).astype(x.dtype)


def rope(x, pos):
    half = HEAD_DIM // 2
    inv = ROPE_THETA ** (-jnp.arange(half, dtype=jnp.float32) / half)
    ang = pos.astype(jnp.float32)[:, None] * inv[None, :]
    cos = jnp.cos(ang)[None, :, None, :]
    sin = jnp.sin(ang)[None, :, None, :]
    xf = x.astype(jnp.float32)
    x1, x2 = xf[..., :half], xf[..., half:]
    return jnp.concatenate([x1 * cos - x2 * sin, x1 * sin + x2 * cos], -1).astype(x.dtype)


def sink_attention(q, k, v, sinks, mask):
    s = jnp.einsum('bnqhgd,bnkhd->bnhgqk', q, k).astype(jnp.float32) * (HEAD_DIM ** -0.5)
    s = jnp.where(mask[None, :, None, None], s, -jnp.inf)
    sink = sinks.astype(jnp.float32).reshape(N_KV_HEADS, Q_GROUP)[None, None, :, :, None, None]
    m = jnp.maximum(jnp.max(s, -1, keepdims=True), sink)
    e = jnp.exp(s - m)
    denom = jnp.sum(e, -1, keepdims=True) + jnp.exp(sink - m)
    p = (e / denom).astype(v.dtype)
    return jnp.einsum('bnhgqk,bnkhd->bnqhgd', p, v)


def attn_prompt(q, k, v, sinks):
    b, l = q.shape[:2]
    nb = l // WINDOW
    qb = q.reshape(b, nb, WINDOW, N_KV_HEADS, Q_GROUP, HEAD_DIM)
    kb = k.reshape(b, nb, WINDOW, N_KV_HEADS, HEAD_DIM)
    vb = v.reshape(b, nb, WINDOW, N_KV_HEADS, HEAD_DIM)
    zk = jnp.zeros_like(kb[:, :1])
    k2 = jnp.concatenate([jnp.concatenate([zk, kb[:, :-1]], 1), kb], 2)
    v2 = jnp.concatenate([jnp.concatenate([zk, vb[:, :-1]], 1), vb], 2)
    qi = jnp.arange(WINDOW)[:, None]
    kj = jnp.arange(2 * WINDOW)[None, :]
    rel = qi + WINDOW - kj
    band = (rel >= 0) & (rel < WINDOW)
    blk = jnp.arange(nb)[:, None, None]
    valid = (blk * WINDOW + kj[None] - WINDOW) >= 0
    mask = band[None] & valid
    o = sink_attention(qb, k2, v2, sinks, mask)
    return o.reshape(b, l, ATTN_WIDTH)


def attn_sample(q, k, v, k_buf, v_buf, sinks):
    b, t = q.shape[:2]
    k2 = jnp.concatenate([k_buf, k], 1)
    v2 = jnp.concatenate([v_buf, v], 1)
    qi = jnp.arange(t)[:, None]
    kj = jnp.arange(WINDOW + t)[None, :]
    rel = qi + WINDOW - kj
    mask = ((rel >= 0) & (rel < WINDOW))[None]
    o = sink_attention(q.reshape(b, 1, t, N_KV_HEADS, Q_GROUP, HEAD_DIM), k2[:, None], v2[:, None], sinks, mask)
    return o.reshape(b, t, ATTN_WIDTH), k2[:, -WINDOW:], v2[:, -WINDOW:]


def ssm_discretize(lam_re, lam_im, log_dt):
    dt = jnp.exp(log_dt.astype(jnp.float32))[:, None]
    lr = lam_re.astype(jnp.float32)
    li = lam_im.astype(jnp.float32)
    mag = jnp.exp(lr * dt)
    a_re = mag * jnp.cos(li * dt)
    a_im = mag * jnp.sin(li * dt)
    nr, ni = a_re - 1.0, a_im
    den = lr * lr + li * li
    c_re = (nr * lr + ni * li) / den
    c_im = (ni * lr - nr * li) / den
    return a_re, a_im, c_re, c_im


def ssm_combine(e1, e2):
    a1r, a1i, b1r, b1i = e1
    a2r, a2i, b2r, b2i = e2
    return (a2r * a1r - a2i * a1i, a2r * a1i + a2i * a1r,
            a2r * b1r - a2i * b1i + b2r, a2r * b1i + a2i * b1r + b2i)


def ssm_branch(u, h0_re, h0_im, lam_re, lam_im, log_dt, b_re, b_im, c_re, c_im, d, w_glu, b_glu):
    bsz, l = u.shape[:2]
    uf = u.astype(jnp.float32).reshape(bsz, l, N_GROUPS, GROUP_SIZE)
    a_re, a_im, k_re, k_im = ssm_discretize(lam_re, lam_im, log_dt)
    bu_re = jnp.einsum('blgc,gpc->blgp', uf, b_re.astype(jnp.float32))
    bu_im = jnp.einsum('blgc,gpc->blgp', uf, b_im.astype(jnp.float32))
    x_re = k_re * bu_re - k_im * bu_im
    x_im = k_re * bu_im + k_im * bu_re
    pr, pi, hr, hi = lax.associative_scan(
        ssm_combine, (jnp.broadcast_to(a_re, x_re.shape), jnp.broadcast_to(a_im, x_re.shape), x_re, x_im), axis=1)
    h0r = h0_re.astype(jnp.float32)[:, None]
    h0i = h0_im.astype(jnp.float32)[:, None]
    hr = hr + pr * h0r - pi * h0i
    hi = hi + pr * h0i + pi * h0r
    y = (jnp.einsum('blgp,gcp->blgc', hr, c_re.astype(jnp.float32))
         - jnp.einsum('blgp,gcp->blgc', hi, c_im.astype(jnp.float32))
         + d.astype(jnp.float32).reshape(N_GROUPS, GROUP_SIZE) * uf)
    y = jax.nn.gelu(y.reshape(bsz, l, SSM_WIDTH), approximate=False)
    out = y * jax.nn.sigmoid(y @ w_glu.astype(jnp.float32) + b_glu.astype(jnp.float32))
    return out.astype(u.dtype), hr[:, -1].astype(u.dtype), hi[:, -1].astype(u.dtype)


def peer(x, w_q, keys1, keys2, u_tab, v_tab):
    shp = x.shape
    xt = x.reshape(-1, D_MODEL)
    t = xt.shape[0]
    nblk = -(-t // PEER_BLOCK)
    xt = jnp.pad(xt, ((0, nblk * PEER_BLOCK - t), (0, 0))).reshape(nblk, PEER_BLOCK, D_MODEL)
    k1 = keys1.astype(jnp.float32)
    k2 = keys2.astype(jnp.float32)

    def one_block(xb):
        q = (xb @ w_q).astype(jnp.float32).reshape(PEER_BLOCK, PEER_HEADS, 2, KEY_DIM)
        s1 = jnp.einsum('thd,hkd->thk', q[:, :, 0], k1)
        s2 = jnp.einsum('thd,hkd->thk', q[:, :, 1], k2)
        v1, i1 = lax.top_k(s1, PEER_TOPK)
        v2, i2 = lax.top_k(s2, PEER_TOPK)
        cand = (v1[..., :, None] + v2[..., None, :]).reshape(PEER_BLOCK, PEER_HEADS, PEER_TOPK * PEER_TOPK)
        sc, ci = lax.top_k(cand, PEER_TOPK)
        e1 = jnp.take_along_axis(i1, ci // PEER_TOPK, -1)
        e2 = jnp.take_along_axis(i2, ci % PEER_TOPK, -1)
        idx = e1 * N_KEYS + e2
        g = jax.nn.softmax(sc, -1)
        ue = u_tab[idx]
        ve = v_tab[idx]
        h = jax.nn.gelu(jnp.einsum('td,thkd->thk', xb, ue).astype(jnp.float32), approximate=False)
        return jnp.einsum('thk,thkd->td', (g * h).astype(ve.dtype), ve).astype(xb.dtype)

    y = lax.map(one_block, xt)
    return y.reshape(-1, D_MODEL)[:t].reshape(shp)


def decoder_layer(x, p, pos, win_k, win_v, h0_re, h0_im,
                  w_in, attn_sinks, w_attn_proj, ssm_lambda_re, ssm_lambda_im, ssm_log_dt,
                  ssm_b_re, ssm_b_im, ssm_c_re, ssm_c_im, ssm_d, w_glu, b_glu, w_ssm_proj,
                  w_out, ln1_g, ln1_b, peer_w_q, peer_keys1, peer_keys2, peer_u, peer_v,
                  ln2_g, ln2_b, ple_w_gate, ple_w_proj):
    b, l, _ = x.shape
    proj = x @ w_in
    q, k, v, u, ga, gb = jnp.split(proj, SPLIT_IDX, axis=-1)
    q = rope(q.reshape(b, l, N_HEADS, HEAD_DIM), pos)
    k = rope(k.reshape(b, l, N_KV_HEADS, HEAD_DIM), pos)
    v = v.reshape(b, l, N_KV_HEADS, HEAD_DIM)
    if win_k is None:
        a = attn_prompt(q, k, v, attn_sinks)
        new_k, new_v = k[:, -WINDOW:], v[:, -WINDOW:]
    else:
        a, new_k, new_v = attn_sample(q, k, v, win_k, win_v, attn_sinks)
    s, hr, hi = ssm_branch(u, h0_re, h0_im, ssm_lambda_re, ssm_lambda_im, ssm_log_dt,
                           ssm_b_re, ssm_b_im, ssm_c_re, ssm_c_im, ssm_d, w_glu, b_glu)
    mix = jax.nn.sigmoid(ga) * (a @ w_attn_proj) + jax.nn.sigmoid(gb) * (s @ w_ssm_proj)
    x = layer_norm(DN_ALPHA * x + mix @ w_out, ln1_g, ln1_b)
    x = layer_norm(DN_ALPHA * x + peer(x, peer_w_q, peer_keys1, peer_keys2, peer_u, peer_v), ln2_g, ln2_b)
    x = x + jax.nn.sigmoid(x @ ple_w_gate) * (p @ ple_w_proj)
    return x, new_k, new_v, hr, hi


def setup_inputs(seed: int = 0) -> dict:
    key = jax.random.key(seed)
    ks = jax.random.split(key, 40)
    f32 = jnp.float32

    def nrm(k, shape, scale):
        return jax.random.normal(k, shape, f32) * scale

    n_idx = jnp.arange(STATE_DIM, dtype=f32)
    ssm_shape = (DEPTH, N_GROUPS, STATE_DIM)
    return {
        'x_prompt': nrm(ks[0], (BATCH, SEQ, D_MODEL), 1.0),
        'x_sample': nrm(ks[1], (DEC_BATCH, DEC_SEQ, D_MODEL), 1.0),
        'p_prompt': nrm(ks[2], (DEPTH, BATCH, SEQ, PLE_DIM), 1.0),
        'p_sample': nrm(ks[3], (DEPTH, DEC_BATCH, DEC_SEQ, PLE_DIM), 1.0),
        'state_win_k': nrm(ks[4], (DEPTH, DEC_BATCH, WINDOW, N_KV_HEADS, HEAD_DIM), 1.0),
        'state_win_v': nrm(ks[5], (DEPTH, DEC_BATCH, WINDOW, N_KV_HEADS, HEAD_DIM), 1.0),
        'state_ssm_re': nrm(ks[6], (DEPTH, DEC_BATCH, N_GROUPS, STATE_DIM), 0.5),
        'state_ssm_im': nrm(ks[7], (DEPTH, DEC_BATCH, N_GROUPS, STATE_DIM), 0.5),
        'w_in': nrm(ks[8], (DEPTH, D_MODEL, IN_WIDTH), D_MODEL ** -0.5),
        'attn_sinks': nrm(ks[9], (DEPTH, N_HEADS), 1.0),
        'w_attn_proj': nrm(ks[10], (DEPTH, ATTN_WIDTH, D_MODEL), ATTN_WIDTH ** -0.5),
        'ssm_lambda_re': -0.5 + nrm(ks[11], ssm_shape, 0.01),
        'ssm_lambda_im': jnp.pi * n_idx + nrm(ks[12], ssm_shape, 0.01),
        'ssm_log_dt': jax.random.uniform(ks[13], (DEPTH, N_GROUPS), f32, math.log(DT_MIN), math.log(DT_MAX)),
        'ssm_b_re': nrm(ks[14], (DEPTH, N_GROUPS, STATE_DIM, GROUP_SIZE), (2 * GROUP_SIZE) ** -0.5),
        'ssm_b_im': nrm(ks[15], (DEPTH, N_GROUPS, STATE_DIM, GROUP_SIZE), (2 * GROUP_SIZE) ** -0.5),
        'ssm_c_re': nrm(ks[16], (DEPTH, N_GROUPS, GROUP_SIZE, STATE_DIM), (2 * STATE_DIM) ** -0.5),
        'ssm_c_im': nrm(ks[17], (DEPTH, N_GROUPS, GROUP_SIZE, STATE_DIM), (2 * STATE_DIM) ** -0.5),
        'ssm_d': nrm(ks[18], (DEPTH, SSM_WIDTH), 1.0),
        'w_glu': nrm(ks[19], (DEPTH, SSM_WIDTH, SSM_WIDTH), SSM_WIDTH ** -0.5),
        'b_glu': nrm(ks[20], (DEPTH, SSM_WIDTH), 0.02),
        'w_ssm_proj': nrm(ks[21], (DEPTH, SSM_WIDTH, D_MODEL), SSM_WIDTH ** -0.5),
        'w_out': nrm(ks[22], (DEPTH, D_MODEL, D_MODEL), DN_BETA * D_MODEL ** -0.5),
        'ln1_g': 1.0 + nrm(ks[23], (DEPTH, D_MODEL), 0.02),
        'ln1_b': nrm(ks[24], (DEPTH, D_MODEL), 0.02),
        'peer_w_q': nrm(ks[25], (DEPTH, D_MODEL, PEER_HEADS * 2 * KEY_DIM), D_MODEL ** -0.5),
        'peer_keys1': nrm(ks[26], (DEPTH, PEER_HEADS, N_KEYS, KEY_DIM), KEY_DIM ** -0.5),
        'peer_keys2': nrm(ks[27], (DEPTH, PEER_HEADS, N_KEYS, KEY_DIM), KEY_DIM ** -0.5),
        'peer_u': nrm(ks[28], (DEPTH, N_EXPERTS, D_MODEL), D_MODEL ** -0.5),
        'peer_v': nrm(ks[29], (DEPTH, N_EXPERTS, D_MODEL), DN_BETA * PEER_HEADS ** -0.5),
        'ln2_g': 1.0 + nrm(ks[30], (DEPTH, D_MODEL), 0.02),
        'ln2_b': nrm(ks[31], (DEPTH, D_MODEL), 0.02),
        'ple_w_gate': nrm(ks[32], (DEPTH, D_MODEL, D_MODEL), D_MODEL ** -0.5),
        'ple_w_proj': nrm(ks[33], (DEPTH, PLE_DIM, D_MODEL), PLE_DIM ** -0.5),
    }


def reference(x_prompt, x_sample, p_prompt, p_sample, state_win_k, state_win_v, state_ssm_re, state_ssm_im,
              w_in, attn_sinks, w_attn_proj, ssm_lambda_re, ssm_lambda_im, ssm_log_dt,
              ssm_b_re, ssm_b_im, ssm_c_re, ssm_c_im, ssm_d, w_glu, b_glu, w_ssm_proj,
              w_out, ln1_g, ln1_b, peer_w_q, peer_keys1, peer_keys2, peer_u, peer_v,
              ln2_g, ln2_b, ple_w_gate, ple_w_proj):
    pos_p = jnp.arange(x_prompt.shape[1], dtype=jnp.int32)
    pos_s = PAST_LEN + jnp.arange(x_sample.shape[1], dtype=jnp.int32)
    h_zero = jnp.zeros((x_prompt.shape[0], N_GROUPS, STATE_DIM), x_prompt.dtype)
    yp, ys = x_prompt, x_sample
    kp_l, vp_l, rp_l, ip_l, ks_l, vs_l, rs_l, is_l = [], [], [], [], [], [], [], []
    for i in range(DEPTH):
        w = (w_in[i], attn_sinks[i], w_attn_proj[i], ssm_lambda_re[i], ssm_lambda_im[i], ssm_log_dt[i],
             ssm_b_re[i], ssm_b_im[i], ssm_c_re[i], ssm_c_im[i], ssm_d[i], w_glu[i], b_glu[i], w_ssm_proj[i],
             w_out[i], ln1_g[i], ln1_b[i], peer_w_q[i], peer_keys1[i], peer_keys2[i], peer_u[i], peer_v[i],
             ln2_g[i], ln2_b[i], ple_w_gate[i], ple_w_proj[i])
        yp, kp, vp, rp, ip = decoder_layer(yp, p_prompt[i], pos_p, None, None, h_zero, h_zero, *w)
        ys, k_s, v_s, r_s, i_s = decoder_layer(ys, p_sample[i], pos_s, state_win_k[i], state_win_v[i],
                                               state_ssm_re[i], state_ssm_im[i], *w)
        kp_l.append(kp); vp_l.append(vp); rp_l.append(rp); ip_l.append(ip)
        ks_l.append(k_s); vs_l.append(v_s); rs_l.append(r_s); is_l.append(i_s)
    new_win_k_prompt = jnp.stack(kp_l)
    new_win_v_prompt = jnp.stack(vp_l)
    new_ssm_re_prompt = jnp.stack(rp_l)
    new_ssm_im_prompt = jnp.stack(ip_l)
    new_win_k_sample = jnp.stack(ks_l)
    new_win_v_sample = jnp.stack(vs_l)
    new_ssm_re_sample = jnp.stack(rs_l)
    new_ssm_im_sample = jnp.stack(is_l)
    return (yp, ys, new_win_k_prompt, new_win_v_prompt, new_ssm_re_prompt, new_ssm_im_prompt,
            new_win_k_sample, new_win_v_sample, new_ssm_re_sample, new_ssm_im_sample)
```

```python
from contextlib import ExitStack
import numpy as np
import concourse.bass as bass
import concourse.mybir as mybir
from concourse.bass_utils import run_bass_kernel_spmd

F32 = mybir.dt.float32
BF16 = mybir.dt.bfloat16
I32 = mybir.dt.int32
U32 = mybir.dt.uint32
AF = mybir.ActivationFunctionType
ALU = mybir.AluOpType
AX = mybir.AxisListType

NCORES = 8
D = 1024
TP = 4096
TS = 128
TT = TP + TS
NTILE = TT // 128
NSEQ = 16
DSEQ = 8
ALPHA = 2.0 ** 0.25
LN_EPS = 1e-5
NEG = -30000.0
QO, QP, KO, KP, UO, GA, GB, VO, WIN = 0, 512, 1024, 1152, 1280, 1792, 2816, 3840, 3968
TWO_PI = float(2.0 * np.pi)
PI = float(np.pi)

ENGS = ["pe", "act", "dve", "pool", "sp"]


class Buf:
    def __init__(self, name, t):
        self.name = name
        self.t = t
        self.w = None
        self.r = []
        self.dsem = None
        self.dcount = 0

    def __getitem__(self, idx):
        return self.t[idx]


class Prog:
    def __init__(self, nc):
        self.nc = nc
        self.q = {e: [] for e in ENGS}
        self.cnt = {e: 0 for e in ENGS}
        self.semkeys = [f"e_{e}" for e in ENGS]
        self.waited = {e: {} for e in ENGS}
        self.ndsem = 0
        self.stack = None
        self.final = []
        self.dma_last = {}
        self.pool_out = []

    def sbuf(self, name, shape, dt):
        return Buf(name, self.stack.enter_context(self.nc.sbuf_tensor(name, list(shape), dt)))

    def psum(self, name, shape, dt):
        return Buf(name, self.stack.enter_context(self.nc.psum_tensor(name, list(shape), dt)))

    def dram(self, name, shape, dt, kind="Internal"):
        return Buf(name, self.nc.dram_tensor(name, list(shape), dt, kind=kind))

    def _need(self, eng, tok, same_ok=False):
        if tok is None:
            return
        key, val = tok
        if same_ok and key == f"e_{eng}":
            return
        if self.waited[eng].get(key, 0) >= val:
            return
        self.waited[eng][key] = val
        self.q[eng].append(("wait", key, val))

    def _deps(self, eng, reads, writes, pe_accum=False):
        same_ok = pe_accum or eng in ("dve", "act")
        for b in reads:
            self._need(eng, b.w)
        for b in writes:
            self._need(eng, b.w, same_ok=same_ok)
            for tk in b.r:
                self._need(eng, tk, same_ok=same_ok)

    def _mark(self, tok, reads, writes):
        for b in reads:
            if b not in writes:
                b.r.append(tok)
                if len(b.r) > 16:
                    m = {}
                    for k, v in b.r:
                        m[k] = max(m.get(k, 0), v)
                    b.r = list(m.items())
        for b in writes:
            b.w = tok
            b.r = []

    def op(self, eng, fn, reads=(), writes=(), pe_accum=False):
        self._deps(eng, reads, writes, pe_accum)
        self.cnt[eng] += 1
        tok = (f"e_{eng}", self.cnt[eng])
        self.q[eng].append(("op", fn, tok[0], 1))
        self._mark(tok, reads, writes)
        return tok

    def dma(self, eng, fn, owner, reads=(), writes=(), final=False, nd=64):
        if eng == "pool":
            while self.pool_out and sum(n for _, n in self.pool_out) + nd > 700:
                tk, _ = self.pool_out.pop(0)
                self._need("pool", tk)
        self._deps(eng, reads, writes)
        if owner.dsem is None:
            owner.dsem = f"d{self.ndsem}_{owner.name}"
            self.ndsem += 1
            self.semkeys.append(owner.dsem)
        owner.dcount += 16
        tok = (owner.dsem, owner.dcount)
        self.dma_last[owner.dsem] = owner.dcount
        self.q[eng].append(("op", fn, tok[0], 16))
        if eng == "pool":
            self.pool_out.append((tok, nd))
        self._mark(tok, reads, writes)
        if final:
            self.final.append(tok)
        return tok

    def barrier(self):
        toks = [(f"e_{e}", self.cnt[e]) for e in ENGS if self.cnt[e] > 0]
        toks += [(k, v) for k, v in self.dma_last.items()]
        for e in ENGS:
            for tk in toks:
                self._need(e, tk)

    def emit(self, stack):
        nc = self.nc
        for tok in self.final:
            self._need("sp", tok)
        needed = {f"e_{e}": set() for e in ENGS}
        for e in ENGS:
            for it in self.q[e]:
                if it[0] == "wait" and it[1] in needed:
                    needed[it[1]].add(it[2])
        rank = {k: {v: i + 1 for i, v in enumerate(sorted(vs))} for k, vs in needed.items()}
        sems = {k: stack.enter_context(nc.semaphore(k)) for k in self.semkeys}
        block = stack.enter_context(nc.Block())

        def run(engname):
            ekey = f"e_{engname}"

            def body(e):
                n = 0
                for it in self.q[engname]:
                    if it[0] == "wait":
                        val = rank[it[1]][it[2]] if it[1] in rank else it[2]
                        e.wait_ge(sems[it[1]], val)
                    else:
                        ins = it[1](e)
                        if it[2] == ekey:
                            n += 1
                            if n in needed[ekey]:
                                ins.then_inc(sems[ekey], 1)
                        else:
                            ins.then_inc(sems[it[2]], it[3])
            return body

        block.tensor(run("pe"))
        block.scalar(run("act"))
        block.vector(run("dve"))
        block.gpsimd(run("pool"))
        block.sync(run("sp"))


class KB:
    def __init__(self, tiles=None, phase_b=True, debug=False, NT=1, EC=2, phase_a=True, NB=3, NBUF=3, BGR=1.6, mask_eng="pool", post_eng="dve", BPOOL=0):
        self.NT, self.EC, self.do_a, self.NB, self.NBUF, self.BGR, self.mask_eng, self.post_eng = NT, EC, phase_a, NB, NBUF, BGR, mask_eng, post_eng
        self.BPOOL = BPOOL
        self.npool = 6
        self.tiles = list(range(NTILE)) if tiles is None else tiles
        self.phase_b = phase_b
        self.debug = debug
        self.nc = bass.Bass("TRN2", target_bir_lowering=False)
        self.P = Prog(self.nc)
        self.dbg_outs = []

    def mm(self, ob, oap, lb, lap, rb, rap, start=True, stop=True):
        self.P.op("pe", lambda e: e.matmul(oap, lhsT=lap, rhs=rap, start=start, stop=stop),
                  reads=[lb, rb], writes=[ob], pe_accum=True)

    def tt(self, eng, ob, oap, ab, aap, bb, bap, op):
        self.P.op(eng, lambda e: e.tensor_tensor(out=oap, in0=aap, in1=bap, op=op), reads=[ab, bb], writes=[ob])

    def ts(self, eng, ob, oap, ab, aap, s1, s2, op0, op1=None, extra_reads=()):
        if op1 is None:
            self.P.op(eng, lambda e: e.tensor_scalar(out=oap, in0=aap, scalar1=s1, scalar2=None, op0=op0),
                      reads=[ab] + list(extra_reads), writes=[ob])
        else:
            self.P.op(eng, lambda e: e.tensor_scalar(out=oap, in0=aap, scalar1=s1, scalar2=s2, op0=op0, op1=op1),
                      reads=[ab] + list(extra_reads), writes=[ob])

    def act(self, ob, oap, ab, aap, func, scale=1.0, bias=0.0, extra_reads=()):
        self.P.op("act", lambda e: e.activation(out=oap, in_=aap, func=func, scale=scale, bias=bias),
                  reads=[ab] + list(extra_reads), writes=[ob])

    def cp(self, eng, ob, oap, ab, aap):
        if eng == "act":
            self.P.op("act", lambda e: e.copy(out=oap, in_=aap), reads=[ab], writes=[ob])
        else:
            self.P.op(eng, lambda e: e.tensor_copy(out=oap, in_=aap), reads=[ab], writes=[ob])

    def load(self, q, db, dap, sb, sap, nd=64):
        self.P.dma(q, lambda e: e.dma_start(out=dap, in_=sap), owner=db, reads=[sb], writes=[db], nd=nd)

    def store(self, q, db, dap, sb, sap, final=True):
        self.P.dma(q, lambda e: e.dma_start(out=dap, in_=sap), owner=sb, reads=[sb], writes=[db], final=final)

    def nb(self):
        b = self.banks[self.bank_i % self.npool]
        self.bank_i += 1
        return b

    def dbg(self, name, buf, ap, shape, dt=F32):
        if not self.debug:
            return
        o = self.P.dram(name, shape, dt, kind="ExternalOutput")
        self.dbg_outs.append(name)
        self.store("sp", o, o[:], buf, ap)

    def build(self):
        P = self.P
        with ExitStack() as st:
            P.stack = st
            self.declare_io()
            self.alloc_common()
            if self.phase_b:
                self.convert_tables()
            if self.do_a:
                with ExitStack() as sa:
                    P.stack = sa
                    self.phase_a()
                    P.barrier()
            if self.phase_b:
                with ExitStack() as sbk:
                    P.stack = sbk
                    self.phase_b_run()
                    P.barrier()
            P.stack = st
            P.emit(st)
        return self.nc

    def declare_io(self):
        P = self.P
        I = lambda n, s: P.dram(n, s, F32, kind="ExternalInput")
        O = lambda n, s: P.dram(n, s, F32, kind="ExternalOutput")
        self.xT = I("xT", [D, TT])
        self.x = I("x", [TT, D])
        self.pT = I("pT", [256, TT])
        self.w_in = I("w_in", [D, WIN])
        self.ropeC = I("ropeC", [128, TT])
        self.ropeS = I("ropeS", [128, TT])
        self.ropeCt = I("ropeCt", [256, 32])
        self.ropeSt = I("ropeSt", [256, 32])
        self.sinks = I("sinks", [128, 8])
        self.w_ap = I("w_ap", [512, D])
        self.w_sp = I("w_sp", [512, D])
        self.w_glu = I("w_glu", [512, 512])
        self.b_glu = I("b_glu", [128, 4])
        self.w_out = I("w_out", [D, D])
        self.ln1g = I("ln1g", [128, D])
        self.ln1b = I("ln1b", [128, D])
        self.lam_re = I("lam_re", [128, 16])
        self.lam_im = I("lam_im", [128, 16])
        self.logdt = I("logdt", [128, 16])
        self.Bre = I("Bre", [128, 16, 16])
        self.Bim = I("Bim", [128, 16, 16])
        self.Cre = I("Cre", [128, 16, 16])
        self.Cim = I("Cim", [128, 16, 16])
        self.dssm = I("dssm", [128, 4])
        self.wkT = I("wkT", [128, NSEQ, 128])
        self.wv = I("wv", [128, NSEQ, 128])
        self.wk_raw = I("wk_raw", [NSEQ, 128, 128])
        self.wv_raw = I("wv_raw", [NSEQ, 128, 128])
        self.h0re = I("h0re", [128, 16, NSEQ])
        self.h0im = I("h0im", [128, 16, NSEQ])
        self.mask_own = I("mask_own", [128, 512])
        self.mask_prev = I("mask_prev", [128, 512])
        self.mask_new = I("mask_new", [128, 512])
        self.mask_past = I("mask_past", [128, 32])
        self.iota_tau = I("iota_tau", [128, 128])
        self.w_q = I("w_q", [D, 2048])
        self.keysT = I("keysT", [128, 16, 128])
        self.uT = I("uT", [D, 16384])
        self.vtab = I("vtab", [16384, D])
        self.ln2g = I("ln2g", [128, D])
        self.ln2b = I("ln2b", [128, D])
        self.w_gate = I("w_gate", [D, D])
        self.w_pp = I("w_pp", [256, D])
        self.iota128 = I("iota128", [128, 128])
        self.iota16 = I("iota16", [128, 16])
        self.y = O("y", [TT, D])
        self.nk_p = O("nk_p", [128, 128])
        self.nv_p = O("nv_p", [128, 128])
        self.hre_p = O("hre_p", [128, 16])
        self.him_p = O("him_p", [128, 16])
        self.nk_s = O("nk_s", [NSEQ, 128, 128])
        self.nv_s = O("nv_s", [NSEQ, 128, 128])
        self.hre_s = O("hre_s", [128, 16, NSEQ])
        self.him_s = O("him_s", [128, 16, NSEQ])
        self.X1 = [P.dram(f"X1_{t}", [128, D], F32, kind=("ExternalOutput" if (self.debug and t == 0) else "Internal"))
                   for t in range(NTILE)]

    def alloc_common(self):
        P = self.P
        self.banks = [P.psum(f"bank{i}", [128, 512], F32) for i in range(8)]
        self.bank_i = 0
        self.identb = P.sbuf("identb", [128, 128], BF16)
        self.identf = P.sbuf("identf", [128, 128], F32)
        self.onesb = P.sbuf("onesb", [128, 64], BF16)
        for t_, in [(self.identb,), (self.identf,)]:
            P.op("pool", lambda e, t_=t_: e.memset(t_[:], 1.0), writes=[t_])
            P.op("pool", lambda e, t_=t_: e.affine_select(out=t_[:], in_=t_[:], pattern=[[-1, 128]],
                                                          compare_op=ALU.is_equal, fill=0.0, base=0,
                                                          channel_multiplier=1), reads=[t_], writes=[t_])
        P.op("pool", lambda e: e.memset(self.onesb[:], 1.0), writes=[self.onesb])

    def load_cast(self, db, dap_fn, sb, sap_fn, ncols, maxc=2048):
        c0 = 0
        while c0 < ncols:
            c1 = min(ncols, c0 + maxc)
            self.load("pool", db, dap_fn(c0, c1), sb, sap_fn(c0, c1))
            c0 = c1

    def phase_a(self):
        P = self.P
        sb = P.sbuf
        self.winb = sb("winb", [128, 8, WIN], BF16)
        self.load_cast(self.winb, lambda a, b: self.winb[:, :, a:b], self.w_in,
                       lambda a, b: self.w_in[:, a:b].rearrange("(k p) n -> p k n", p=128), WIN, 1984)
        self.wapb = sb("wapb", [64, 8, D], BF16)
        self.load("pool", self.wapb, self.wapb[:], self.w_ap, self.w_ap[:].rearrange("(h p) n -> p h n", p=64))
        self.wspb = sb("wspb", [128, 4, D], BF16)
        self.load("pool", self.wspb, self.wspb[:], self.w_sp, self.w_sp[:].rearrange("(k p) n -> p k n", p=128))
        self.wglub = sb("wglub", [128, 4, 512], BF16)
        self.load("pool", self.wglub, self.wglub[:], self.w_glu, self.w_glu[:].rearrange("(k p) n -> p k n", p=128))
        self.woutb = sb("woutb", [128, 8, D], BF16)
        self.load("pool", self.woutb, self.woutb[:], self.w_out, self.w_out[:].rearrange("(k p) n -> p k n", p=128))
        self.bglu = sb("bglu", [128, 4], F32)
        self.load("sp", self.bglu, self.bglu[:], self.b_glu, self.b_glu[:])
        self.dss = sb("dss", [128, 4], F32)
        self.load("sp", self.dss, self.dss[:], self.dssm, self.dssm[:])
        self.g1 = sb("g1", [128, D], F32)
        self.b1 = sb("b1", [128, D], F32)
        self.load("sp", self.g1, self.g1[:], self.ln1g, self.ln1g[:])
        self.load("sp", self.b1, self.b1[:], self.ln1b, self.ln1b[:])
        self.mown = sb("mown", [128, 512], BF16)
        self.mprev = sb("mprev", [128, 512], BF16)
        self.mnew = sb("mnew", [128, 512], BF16)
        self.mpast = sb("mpast", [128, 32], BF16)
        for d_, s_ in [(self.mown, self.mask_own), (self.mprev, self.mask_prev), (self.mnew, self.mask_new),
                       (self.mpast, self.mask_past)]:
            self.load("pool", d_, d_[:], s_, s_[:])
        self.esink = sb("esink", [128, 8], F32)
        self.load("sp", self.esink, self.esink[:], self.sinks, self.sinks[:])
        self.act(self.esink, self.esink[:], self.esink, self.esink[:], AF.Exp)
        self.ssm_prologue()
        self.xTb = sb("xTb", [128, 8, 128], BF16)
        self.xtok = sb("xtok", [128, D], F32)
        self.rc = sb("rc", [128, 128], F32)
        self.rs = sb("rs", [128, 128], F32)
        self.qTb = sb("qTb", [128, 4, 128], BF16)
        self.kTb = [sb(f"kTb{i}", [128, 128], BF16) for i in range(2)]
        self.vb = [sb(f"vb{i}", [128, 128], BF16) for i in range(2)]
        self.vf = sb("vf", [128, 128], F32)
        self.ktok = sb("ktok", [128, 2, 2, 32], F32)
        self.rct = sb("rct", [128, 32], F32)
        self.rst = sb("rst", [128, 32], F32)
        self.uTbs = [sb(f"uTb{i}", [128, 4, 128], BF16) for i in range(2)]
        self.uTb = self.uTbs[0]
        self.sga = sb("sga", [128, 8, 128], BF16)
        self.sgb = sb("sgb", [128, 8, 128], BF16)
        self.tA = sb("tA", [128, 512], F32)
        self.tB = sb("tB", [128, 512], F32)
        self.tC = sb("tC", [128, 512], F32)
        self.tD = sb("tD", [128, 512], F32)
        self.pA, self.pB = self.tA, self.tB
        self.pTo2 = [sb(f"pTo{i}", [128, 512], BF16) for i in range(2)]
        self.pTp2 = [sb(f"pTp{i}", [128, 512], BF16) for i in range(2)]
        self.pTo, self.pTp = self.pTo2[0], self.pTp2[0]
        self.aTb = sb("aTb", [64, 8, 128], BF16)
        self.rden = sb("rden", [64, 512], F32)
        self.numS = sb("numS", [64, 512], F32)
        self.xhr = sb("xhr", [128, 512], F32)
        self.xhi = sb("xhi", [128, 512], F32)
        self.ghr = sb("ghr", [128, 512], F32)
        self.ghi = sb("ghi", [128, 512], F32)
        self.hreb = sb("hreb", [128, 4, 128], BF16)
        self.himb = sb("himb", [128, 4, 128], BF16)
        self.carr = sb("carr", [128, 16], F32)
        self.cari = sb("cari", [128, 16], F32)
        self.gini_r = sb("gini_r", [128, 16], F32)
        self.gini_i = sb("gini_i", [128, 16], F32)
        self.c1 = sb("c1", [128, 16], F32)
        self.c2 = sb("c2", [128, 16], F32)
        self.c3 = sb("c3", [128, 16], F32)
        self.c4 = sb("c4", [128, 16], F32)
        self.ygb = sb("ygb", [128, 4, 128], BF16)
        self.sig = sb("sig", [128, 4, 128], BF16)
        self.sTb = sb("sTb", [128, 4, 128], BF16)
        self.mixb = sb("mixb", [128, 8, 128], BF16)
        self.mA = sb("mA", [128, 512], BF16)
        self.mB = sb("mB", [128, 512], BF16)
        self.x1 = sb("x1", [128, D], F32)
        self.r1 = self.x1
        self.stats = sb("stats", [128, 2, 6], F32)
        self.mv = sb("mv", [128, 2], F32)
        self.rstd = sb("rstd", [128, 1], F32)
        outer = P.stack
        with ExitStack() as pst:
            P.stack = pst
            self.Tc, self.Ts = sb("Tc", [128, 16, 128], F32), sb("Ts", [128, 16, 128], F32)
            print("phase A (prompt) sbuf bytes remaining:", self.nc.sbuf_bytes_remaining)
            self.ssm_prologue_run()
            P.op("dve", lambda e: e.memset(self.carr[:], 0.0), writes=[self.carr])
            P.op("dve", lambda e: e.memset(self.cari[:], 0.0), writes=[self.cari])
            self.run_tiles_a([t for t in self.tiles if t < NTILE - 1])
            P.barrier()
        P.stack = outer
        self.kbT = sb("kbT", [128, NSEQ, 128], BF16)
        self.vbuf = sb("vbuf", [128, NSEQ, 128], BF16)
        self.h0r = sb("h0r", [128, 16, NSEQ], F32)
        self.h0i = sb("h0i", [128, 16, NSEQ], F32)
        self.hfr = sb("hfr", [128, 16, NSEQ], F32)
        self.hfi = sb("hfi", [128, 16, NSEQ], F32)
        self.s1 = sb("s1", [128, 4, NSEQ], F32)
        self.s2 = sb("s2", [128, 4, NSEQ], F32)
        print("phase A sbuf bytes remaining:", self.nc.sbuf_bytes_remaining)
        if NTILE - 1 in self.tiles:
            self.run_tiles_a([NTILE - 1])

    def range_reduce2(self, angB, ang, tfB, tf, tiB, ti):
        P = self.P
        self.ts("dve", tfB, tf, angB, ang, 1.0 / TWO_PI, None, ALU.mult)
        self.cp("dve", tiB, ti, tfB, tf)
        self.cp("dve", tfB, tf, tiB, ti)
        P.op("dve", lambda e: e.scalar_tensor_tensor(out=ang, in0=tf, scalar=-TWO_PI, in1=ang,
                                                     op0=ALU.mult, op1=ALU.add), reads=[tfB, angB], writes=[angB])
        self.ts("dve", tfB, tf, angB, ang, PI, -TWO_PI, ALU.is_gt, ALU.mult)
        self.tt("dve", angB, ang, angB, ang, tfB, tf, ALU.add)
        self.ts("dve", tfB, tf, angB, ang, -PI, TWO_PI, ALU.is_lt, ALU.mult)
        self.tt("dve", angB, ang, angB, ang, tfB, tf, ALU.add)
        self.ts("dve", angB, ang, angB, ang, 3.14159, -3.14159, ALU.min, ALU.max)

    def range_reduce(self, ang, shape, tmpf, tmpi):
        P = self.P
        self.ts("dve", tmpf, tmpf[:], ang, ang[:], 1.0 / TWO_PI, None, ALU.mult)
        self.cp("dve", tmpi, tmpi[:], tmpf, tmpf[:])
        self.cp("dve", tmpf, tmpf[:], tmpi, tmpi[:])
        P.op("dve", lambda e: e.scalar_tensor_tensor(out=ang[:], in0=tmpf[:], scalar=-TWO_PI, in1=ang[:],
                                                     op0=ALU.mult, op1=ALU.add), reads=[tmpf, ang], writes=[ang])
        self.ts("dve", tmpf, tmpf[:], ang, ang[:], PI, -TWO_PI, ALU.is_gt, ALU.mult)
        self.tt("dve", ang, ang[:], ang, ang[:], tmpf, tmpf[:], ALU.add)
        self.ts("dve", tmpf, tmpf[:], ang, ang[:], -PI, TWO_PI, ALU.is_lt, ALU.mult)
        self.tt("dve", ang, ang[:], ang, ang[:], tmpf, tmpf[:], ALU.add)
        self.ts("dve", ang, ang[:], ang, ang[:], 3.14159, -3.14159, ALU.min, ALU.max)

    def ssm_prologue(self):
        P = self.P
        S = [128, 16]
        S4 = [128, 16, 128]
        self.mag, self.sth, self.cth = P.sbuf("mag", S, F32), P.sbuf("sth", S, F32), P.sbuf("cth", S, F32)
        self.are, self.aim = P.sbuf("are", S, F32), P.sbuf("aim", S, F32)
        self.KBT = [P.sbuf("KBTre", S4, BF16), P.sbuf("KBTim", S4, BF16)]
        self.CT = [P.sbuf("CTre", S4, BF16), P.sbuf("CTimn", S4, BF16)]

    def ssm_prologue_run(self):
        P = self.P
        outer = P.stack
        with ExitStack() as tmp:
            P.stack = tmp
            self._ssm_prologue_body()
            P.barrier()
        P.stack = outer

    def _ssm_prologue_body(self):
        P = self.P
        sb = P.sbuf
        S = [128, 16]
        lr, li, ld = sb("lr", S, F32), sb("li", S, F32), sb("ld", S, F32)
        for d_, s_ in [(lr, self.lam_re), (li, self.lam_im), (ld, self.logdt)]:
            self.load("sp", d_, d_[:], s_, s_[:])
        dt = sb("dt", S, F32)
        self.act(dt, dt[:], ld, ld[:], AF.Exp)
        t0, t1, t2 = sb("p_t0", S, F32), sb("p_t1", S, F32), sb("p_t2", S, F32)
        ti = sb("p_ti", S, I32)
        self.tt("dve", t0, t0[:], lr, lr[:], dt, dt[:], ALU.mult)
        self.act(self.mag, self.mag[:], t0, t0[:], AF.Exp)
        th = sb("th", S, F32)
        self.tt("dve", th, th[:], li, li[:], dt, dt[:], ALU.mult)
        ang = sb("p_ang", S, F32)
        self.cp("dve", ang, ang[:], th, th[:])
        self.range_reduce(ang, S, t1, ti)
        self.act(self.sth, self.sth[:], ang, ang[:], AF.Sin)
        self.ts("dve", ang, ang[:], th, th[:], PI / 2.0, None, ALU.add)
        self.range_reduce(ang, S, t1, ti)
        self.act(self.cth, self.cth[:], ang, ang[:], AF.Sin)
        self.tt("dve", self.are, self.are[:], self.mag, self.mag[:], self.cth, self.cth[:], ALU.mult)
        self.tt("dve", self.aim, self.aim[:], self.mag, self.mag[:], self.sth, self.sth[:], ALU.mult)
        nr, den, kre, kim = sb("nr", S, F32), sb("den", S, F32), sb("kre", S, F32), sb("kim", S, F32)
        self.ts("dve", nr, nr[:], self.are, self.are[:], -1.0, None, ALU.add)
        self.tt("dve", t0, t0[:], lr, lr[:], lr, lr[:], ALU.mult)
        self.tt("dve", t1, t1[:], li, li[:], li, li[:], ALU.mult)
        self.tt("dve", den, den[:], t0, t0[:], t1, t1[:], ALU.add)
        P.op("dve", lambda e: e.reciprocal(out=den[:], in_=den[:]), reads=[den], writes=[den])
        self.tt("dve", t0, t0[:], nr, nr[:], lr, lr[:], ALU.mult)
        self.tt("dve", t1, t1[:], self.aim, self.aim[:], li, li[:], ALU.mult)
        self.tt("dve", t2, t2[:], t0, t0[:], t1, t1[:], ALU.add)
        self.tt("dve", kre, kre[:], t2, t2[:], den, den[:], ALU.mult)
        self.tt("dve", t0, t0[:], self.aim, self.aim[:], lr, lr[:], ALU.mult)
        self.tt("dve", t1, t1[:], nr, nr[:], li, li[:], ALU.mult)
        self.tt("dve", t2, t2[:], t0, t0[:], t1, t1[:], ALU.subtract)
        self.tt("dve", kim, kim[:], t2, t2[:], den, den[:], ALU.mult)
        S3 = [128, 16, 16]
        r3 = lambda ap: ap.rearrange("p (s c) -> p s c", s=16)
        brB, br = self.xhr, r3(self.xhr[:, 0:256])
        biB, bi = self.xhr, r3(self.xhr[:, 256:512])
        kbrB, kbr = self.xhi, r3(self.xhi[:, 0:256])
        kbiB, kbi = self.xhi, r3(self.xhi[:, 256:512])
        u0B, u0 = self.ghr, r3(self.ghr[:, 0:256])
        u1B, u1 = self.ghr, r3(self.ghr[:, 256:512])
        crB, cr = self.ghi, r3(self.ghi[:, 0:256])
        ciB, ci = self.ghi, r3(self.ghi[:, 256:512])
        self.load("sp", brB, br, self.Bre, self.Bre[:])
        self.load("sp", biB, bi, self.Bim, self.Bim[:])
        krb = kre[:].unsqueeze(2).to_broadcast(S3)
        kib = kim[:].unsqueeze(2).to_broadcast(S3)
        self.tt("dve", u0B, u0, brB, br, kre, krb, ALU.mult)
        self.tt("dve", u1B, u1, biB, bi, kim, kib, ALU.mult)
        self.tt("dve", kbrB, kbr, u0B, u0, u1B, u1, ALU.subtract)
        self.tt("dve", u0B, u0, biB, bi, kre, krb, ALU.mult)
        self.tt("dve", u1B, u1, brB, br, kim, kib, ALU.mult)
        self.tt("dve", kbiB, kbi, u0B, u0, u1B, u1, ALU.add)
        padB = self.xtok
        pad = self.xtok[:].bitcast(BF16).rearrange("p (s n) -> p s n", s=16)
        for ri, (srcB, src) in enumerate([(kbrB, kbr), (kbiB, kbi)]):
            P.op("dve", lambda e: e.memset(pad, 0.0), writes=[padB])
            for qd in range(4):
                for hf in range(2):
                    c0 = 32 * qd + 16 * hf
                    self.cp("dve", padB, pad[64 * hf:64 * hf + 64, qd::4, c0:c0 + 16], srcB,
                            src[64 * hf:64 * hf + 64, qd::4, :])
            for grp in range(2):
                bk = self.nb()
                bkb = bk[:].bitcast(BF16)
                for j in range(8):
                    stt = grp * 8 + j
                    P.op("pe", lambda e, stt=stt, j=j, bkb=bkb: e.transpose(out=bkb[:, j * 128:(j + 1) * 128],
                                                                            in_=pad[:, stt, :], identity=self.identb[:]),
                         reads=[padB, self.identb], writes=[bk], pe_accum=True)
                self.cp("act", self.KBT[ri], self.KBT[ri][:, grp * 8:(grp + 1) * 8, :], bk,
                        bkb.rearrange("p (j n) -> p j n", j=8))
        self.load("sp", crB, cr, self.Cre, self.Cre[:])
        self.load("sp", ciB, ci, self.Cim, self.Cim[:])
        self.ts("dve", ciB, ci, ciB, ci, -1.0, None, ALU.mult)
        for ri, (srcB, src) in enumerate([(crB, cr), (ciB, ci)]):
            P.op("dve", lambda e, ri=ri: e.memset(self.CT[ri][:], 0.0), writes=[self.CT[ri]])
            for qd in range(4):
                for hf in range(2):
                    c0 = 32 * qd + 16 * hf
                    self.cp("dve", self.CT[ri], self.CT[ri][64 * hf:64 * hf + 64, qd::4, c0:c0 + 16], srcB,
                            src[64 * hf:64 * hf + 64, qd::4, :])
        io = sb("iotau", [128, 128], F32)
        self.load("sp", io, io[:], self.iota_tau, self.iota_tau[:])
        S4 = [128, 4, 128]
        v3 = lambda b: b[:].rearrange("p (j n) -> p j n", j=4)
        iob = io[:].unsqueeze(1).to_broadcast(S4)
        tiB = self.tC
        for sg in range(4):
            ss = slice(4 * sg, 4 * sg + 4)
            thb = th[:, ss].unsqueeze(2).to_broadcast(S4)
            self.tt("dve", self.tD, v3(self.tD), th, thb, io, iob, ALU.mult)
            self.cp("dve", self.tA, self.tA[:], self.tD, self.tD[:])
            self.range_reduce2(self.tA, self.tA[:], self.tB, self.tB[:], tiB, tiB[:].bitcast(I32))
            self.act(self.Ts, self.Ts[:, ss, :], self.tA, v3(self.tA), AF.Sin)
            self.ts("dve", self.tA, self.tA[:], self.tD, self.tD[:], PI / 2.0, None, ALU.add)
            self.range_reduce2(self.tA, self.tA[:], self.tB, self.tB[:], tiB, tiB[:].bitcast(I32))
            self.act(self.Tc, self.Tc[:, ss, :], self.tA, v3(self.tA), AF.Sin)

    def tile_a_gen(self, t):
        P = self.P
        is_s = (t == NTILE - 1)
        c0 = 128 * t
        cur, prv = t % 2, (t + 1) % 2
        kT, kTprev, vb, vbprev = self.kTb[cur], self.kTb[prv], self.vb[cur], self.vb[prv]
        self.uTb = self.uTbs[cur]
        if self.phase_b:
            self.convert_some(1)
        self.load("pool", self.xTb, self.xTb[:], self.xT, self.xT[:, c0:c0 + 128].rearrange("(k p) n -> p k n", p=128))
        self.load("sp", self.rc, self.rc[:], self.ropeC, self.ropeC[:, c0:c0 + 128])
        self.load("sp", self.rs, self.rs[:], self.ropeS, self.ropeS[:, c0:c0 + 128])

        def proj_fm(bank, j, col):
            for kt in range(8):
                self.mm(bank, bank[:, j * 128:(j + 1) * 128], self.winb, self.winb[:, kt, col:col + 128],
                        self.xTb, self.xTb[:, kt, :], start=(kt == 0), stop=(kt == 7))

        bq, bqp = self.nb(), self.nb()
        for j in range(4):
            proj_fm(bq, j, QO + 128 * j)
        for j in range(4):
            proj_fm(bqp, j, QP + 128 * j)
        rcb = self.rc[:].unsqueeze(1).to_broadcast([128, 4, 128])
        rsb = self.rs[:].unsqueeze(1).to_broadcast([128, 4, 128])
        v3 = lambda b: b[:].rearrange("p (j n) -> p j n", j=4)
        self.tt("dve", self.tA, v3(self.tA), bq, v3(bq), self.rc, rcb, ALU.mult)
        self.tt("dve", self.tB, v3(self.tB), bqp, v3(bqp), self.rs, rsb, ALU.mult)
        self.tt("dve", self.qTb, self.qTb[:], self.tA, v3(self.tA), self.tB, v3(self.tB), ALU.add)
        yield "H"
        bk = self.nb()
        proj_fm(bk, 0, KO)
        proj_fm(bk, 1, KP)
        self.tt("dve", self.tC, self.tC[:, 0:128], bk, bk[:, 0:128], self.rc, self.rc[:], ALU.mult)
        self.tt("dve", self.tD, self.tD[:, 0:128], bk, bk[:, 128:256], self.rs, self.rs[:], ALU.mult)
        self.tt("dve", kT, kT[:], self.tC, self.tC[:, 0:128], self.tD, self.tD[:, 0:128], ALU.add)
        yield "H"
        bv = self.nb()
        for kt in range(8):
            self.mm(bv, bv[:, 0:128], self.xTb, self.xTb[:, kt, :], self.winb, self.winb[:, kt, VO:VO + 128],
                    start=(kt == 0), stop=(kt == 7))
        self.cp("act", vb, vb[:], bv, bv[:, 0:128])
        if t >= NTILE - 2:
            self.cp("act", self.vf, self.vf[:], bv, bv[:, 0:128])
            bkt = self.nb()
            for kt in range(8):
                self.mm(bkt, bkt[:, 0:128], self.xTb, self.xTb[:, kt, :], self.winb, self.winb[:, kt, KO:KO + 128],
                        start=(kt == 0), stop=(kt == 7))
            r0 = 128 * (t - (NTILE - 2))
            self.load("sp", self.rct, self.rct[:], self.ropeCt, self.ropeCt[r0:r0 + 128, :])
            self.load("sp", self.rst, self.rst[:], self.ropeSt, self.ropeSt[r0:r0 + 128, :])
            k4 = bkt[:, 0:128].rearrange("p (g h j) -> p g h j", g=2, h=2)
            cb = self.rct[:].unsqueeze(1).to_broadcast([128, 2, 32])
            sbb = self.rst[:].unsqueeze(1).to_broadcast([128, 2, 32])
            a4 = lambda b: b[:, 0:64].rearrange("p (g j) -> p g j", g=2)
            self.tt("dve", self.tA, a4(self.tA), bkt, k4[:, :, 0, :], self.rct, cb, ALU.mult)
            self.tt("dve", self.tB, a4(self.tB), bkt, k4[:, :, 1, :], self.rst, sbb, ALU.mult)
            self.tt("dve", self.ktok, self.ktok[:, :, 0, :], self.tA, a4(self.tA), self.tB, a4(self.tB), ALU.subtract)
            self.tt("dve", self.tA, a4(self.tA), bkt, k4[:, :, 0, :], self.rst, sbb, ALU.mult)
            self.tt("dve", self.tB, a4(self.tB), bkt, k4[:, :, 1, :], self.rct, cb, ALU.mult)
            self.tt("dve", self.ktok, self.ktok[:, :, 1, :], self.tA, a4(self.tA), self.tB, a4(self.tB), ALU.add)
            kflat = self.ktok[:].rearrange("p g h j -> p (g h j)")
            if not is_s:
                self.store("sp", self.nk_p, self.nk_p[:], self.ktok, kflat)
                self.store("sp", self.nv_p, self.nv_p[:], self.vf, self.vf[:])
            else:
                for s in range(NSEQ):
                    self.store("sp", self.nk_s, self.nk_s[s, 120:128, :], self.ktok, kflat[8 * s:8 * s + 8, :])
                    self.store("sp", self.nv_s, self.nv_s[s, 120:128, :], self.vf, self.vf[8 * s:8 * s + 8, :])
                    P.dma("sp", lambda e, s=s: e.dma_start(out=self.nk_s[s, 0:120, :], in_=self.wk_raw[s, 8:128, :]),
                          owner=self.nk_s, reads=[self.wk_raw], writes=[self.nk_s], final=True)
                    P.dma("sp", lambda e, s=s: e.dma_start(out=self.nv_s[s, 0:120, :], in_=self.wv_raw[s, 8:128, :]),
                          owner=self.nv_s, reads=[self.wv_raw], writes=[self.nv_s], final=True)
        yield "H"
        bu = self.nb()
        for j in range(4):
            proj_fm(bu, j, UO + 128 * j)
        self.cp("act", self.uTb, self.uTb[:], bu, v3(bu))
        yield "END_HEAD"
        self.uTb = self.uTbs[cur]
        self.ssm_tile(t, is_s)
        if is_s:
            self.attn_sample(kT, vb)
        else:
            self.attn_prompt(t, kT, kTprev, vb, vbprev)
        for gi, (col, dst) in enumerate([(GA, self.sga), (GB, self.sgb)]):
            for hh in range(2):
                bg = self.nb()
                for j in range(4):
                    proj_fm(bg, j, col + 512 * hh + 128 * j)
                self.act(dst, dst[:, 4 * hh:4 * hh + 4, :], bg, v3(bg), AF.Sigmoid)
        yield "T2"
        for hh in range(2):
            bA, bS = self.nb(), self.nb()
            for j in range(4):
                dcol = 512 * hh + 128 * j
                for h in range(8):
                    self.mm(bA, bA[:, j * 128:(j + 1) * 128], self.wapb, self.wapb[:, h, dcol:dcol + 128],
                            self.aTb, self.aTb[:, h, :], start=(h == 0), stop=(h == 7))
                for ci in range(4):
                    self.mm(bS, bS[:, j * 128:(j + 1) * 128], self.wspb, self.wspb[:, ci, dcol:dcol + 128],
                            self.sTb, self.sTb[:, ci, :], start=(ci == 0), stop=(ci == 3))
            self.tt("dve", self.mA, v3(self.mA), bA, v3(bA), self.sga, self.sga[:, 4 * hh:4 * hh + 4, :], ALU.mult)
            self.tt("dve", self.mB, v3(self.mB), bS, v3(bS), self.sgb, self.sgb[:, 4 * hh:4 * hh + 4, :], ALU.mult)
            self.tt("dve", self.mixb, self.mixb[:, 4 * hh:4 * hh + 4, :], self.mA, v3(self.mA), self.mB, v3(self.mB), ALU.add)
            yield "T"
        self.load("sp", self.xtok, self.xtok[:], self.x, self.x[c0:c0 + 128, :])
        for hh in range(2):
            by = self.nb()
            for kt in range(8):
                self.mm(by, by[:], self.mixb, self.mixb[:, kt, :], self.woutb, self.woutb[:, kt, 512 * hh:512 * hh + 512],
                        start=(kt == 0), stop=(kt == 7))
            P.op("dve", lambda e, hh=hh, by=by: e.scalar_tensor_tensor(
                out=self.r1[:, 512 * hh:512 * hh + 512], in0=self.xtok[:, 512 * hh:512 * hh + 512], scalar=ALPHA,
                in1=by[:], op0=ALU.mult, op1=ALU.add), reads=[self.xtok, by], writes=[self.r1])
            yield "T"
        self.layer_norm(self.r1, self.x1, self.g1, self.b1)
        self.store("sp", self.X1[t], self.X1[t][:], self.x1, self.x1[:], final=False)
        if self.debug and t in (0, 1, NTILE - 1):
            self.dbg(f"dbg_x1_{t}", self.x1, self.x1[:], [128, D])

    def run_tiles_a(self, tl):
        if not tl:
            return
        gens = {t: self.tile_a_gen(t) for t in tl}

        def run_until(g, marker):
            for m in g:
                if m == marker:
                    return

        run_until(gens[tl[0]], "END_HEAD")
        for i, t in enumerate(tl):
            run_until(gens[t], "T2")
            g2 = gens[t]
            gh = gens[tl[i + 1]] if i + 1 < len(tl) else None
            d2, dh = False, gh is None
            while not (d2 and dh):
                if not d2:
                    try:
                        next(g2)
                    except StopIteration:
                        d2 = True
                if not dh:
                    if next(gh) == "END_HEAD":
                        dh = True

    def layer_norm(self, src, dst, g, b, part=0):
        P = self.P
        stats, mv, rstd = self.stats, self.mv, self.rstd
        for hh in range(2):
            P.op("dve", lambda e, hh=hh: e.bn_stats(out=stats[:, hh, :], in_=src[:, 512 * hh:512 * hh + 512]),
                 reads=[src], writes=[stats])
        P.op("dve", lambda e: e.bn_aggr(out=mv[:], in_=stats[:]), reads=[stats], writes=[mv])
        self.act(rstd, rstd[:], mv, mv[:, 1:2], AF.Sqrt, bias=LN_EPS)
        if part == 1:
            return
        self.layer_norm_fin(src, dst, g, b)

    def layer_norm_fin(self, src, dst, g, b):
        P = self.P
        stats, mv, rstd = self.stats, self.mv, self.rstd
        P.op("dve", lambda e: e.reciprocal(out=rstd[:], in_=rstd[:]), reads=[rstd], writes=[rstd])
        self.ts("dve", dst, dst[:], src, src[:], mv[:, 0:1], rstd[:, 0:1], ALU.subtract, ALU.mult,
                extra_reads=[mv, rstd])
        self.tt("dve", dst, dst[:], dst, dst[:], g, g[:], ALU.mult)
        self.tt("dve", dst, dst[:], dst, dst[:], b, b[:], ALU.add)

    def attn_finish(self, g, bn, bd, extra=None):
        P = self.P
        r3 = lambda ap: ap.rearrange("p (h n) -> p h n", h=4)
        r4 = lambda ap: ap.rearrange("p (h s q) -> p h s q", h=4, s=NSEQ)
        esb = self.esink[0:64, 4 * g:4 * g + 4].unsqueeze(2).to_broadcast([64, 4, 128])
        self.tt("dve", self.rden, r3(self.rden[:]), bd, r3(bd[0:64, :]), self.esink, esb, ALU.add)
        if extra is not None:
            self.tt("dve", self.rden, r4(self.rden[:]), self.rden, r4(self.rden[:]), extra[2], extra[3], ALU.add)
        P.op("dve", lambda e: e.reciprocal(out=self.rden[:], in_=self.rden[:]), reads=[self.rden], writes=[self.rden])
        if extra is None:
            self.tt("dve", self.aTb, self.aTb[:, 4 * g:4 * g + 4, :], bn, r3(bn[0:64, :]), self.rden, r3(self.rden[:]), ALU.mult)
        else:
            self.tt("dve", self.numS, r4(self.numS[:]), bn, r4(bn[0:64, :]), extra[0], extra[1], ALU.add)
            self.tt("dve", self.aTb, self.aTb[:, 4 * g:4 * g + 4, :], self.numS, r3(self.numS[:]), self.rden, r3(self.rden[:]), ALU.mult)

    def attn_prompt(self, t, kT, kTprev, vb, vbprev):
        for g in range(2):
            pr = slice(64 * g, 64 * g + 64)
            bs = self.nb()
            q3 = self.qTb[pr, :, :]
            o3 = bs[:].rearrange("p (h n) -> p h n", h=4)
            self.mm(bs, o3, kT, kT[pr, :], self.qTb, q3, start=True, stop=False)
            self.mm(bs, bs[:], self.identb, self.identb[:], self.mown, self.mown[:], start=False, stop=True)
            self.act(self.pTo2[g], self.pTo2[g][:], bs, bs[:], AF.Exp, scale=0.125)
            if t > 0:
                bs2 = self.nb()
                o32 = bs2[:].rearrange("p (h n) -> p h n", h=4)
                self.mm(bs2, o32, kTprev, kTprev[pr, :], self.qTb, q3, start=True, stop=False)
                self.mm(bs2, bs2[:], self.identb, self.identb[:], self.mprev, self.mprev[:], start=False, stop=True)
                self.act(self.pTp2[g], self.pTp2[g][:], bs2, bs2[:], AF.Exp, scale=0.125)
        for g in range(2):
            pr = slice(64 * g, 64 * g + 64)
            pTo, pTp = self.pTo2[g], self.pTp2[g]
            bn, bd = self.nb(), self.nb()
            self.mm(bn, bn[0:64, :], vb, vb[:, pr], pTo, pTo[:], start=True, stop=(t == 0))
            if t > 0:
                self.mm(bn, bn[0:64, :], vbprev, vbprev[:, pr], pTp, pTp[:], start=False, stop=True)
            self.mm(bd, bd[0:64, :], self.onesb, self.onesb[:], pTo, pTo[:], start=True, stop=(t == 0))
            if t > 0:
                self.mm(bd, bd[0:64, :], self.onesb, self.onesb[:], pTp, pTp[:], start=False, stop=True)
            self.attn_finish(g, bn, bd)

    def attn_sample(self, kT, vb):
        P = self.P
        self.load("pool", self.kbT, self.kbT[:], self.wkT, self.wkT[:], nd=8)
        self.load("pool", self.vbuf, self.vbuf[:], self.wv, self.wv[:], nd=8)
        for g in range(2):
            pr = slice(64 * g, 64 * g + 64)
            q3 = self.qTb[pr, :, :]
            pTo, pTp = self.pTo2[g], self.pTp2[g]
            bs = self.nb()
            self.mm(bs, bs[:].rearrange("p (h n) -> p h n", h=4), kT, kT[pr, :], self.qTb, q3, start=True, stop=False)
            self.mm(bs, bs[:], self.identb, self.identb[:], self.mnew, self.mnew[:], start=False, stop=True)
            self.act(pTo, pTo[:], bs, bs[:], AF.Exp, scale=0.125)
            bp = self.nb()
            for s in range(NSEQ):
                cs = slice(32 * s, 32 * s + 32)
                self.mm(bp, bp[:, cs], self.kbT, self.kbT[pr, s, :], self.qTb, self.qTb[pr, :, 8 * s:8 * s + 8],
                        start=True, stop=False)
                self.mm(bp, bp[:, cs], self.identb, self.identb[:], self.mpast, self.mpast[:], start=False, stop=True)
            self.act(pTp, pTp[:], bp, bp[:], AF.Exp, scale=0.125)
            bn, bd, bn2, bd2 = self.nb(), self.nb(), self.nb(), self.nb()
            self.mm(bn, bn[0:64, :], vb, vb[:, pr], pTo, pTo[:])
            self.mm(bd, bd[0:64, :], self.onesb, self.onesb[:], pTo, pTo[:])
            for s in range(NSEQ):
                cs = slice(32 * s, 32 * s + 32)
                self.mm(bn2, bn2[0:64, cs], self.vbuf, self.vbuf[:, s, pr], pTp, pTp[:, cs])
            self.mm(bd2, bd2[0:64, :], self.onesb, self.onesb[:], pTp, pTp[:])
            self.cp("act", self.tC, self.tC[0:64, :], bn2, bn2[0:64, :])
            self.cp("act", self.tD, self.tD[0:64, :], bd2, bd2[0:64, :])
            perm = lambda ap: ap.rearrange("p (s h q) -> p h s q", s=NSEQ, h=4)
            self.attn_finish(g, bn, bd, extra=(self.tC, perm(self.tC[0:64, :]), self.tD, perm(self.tD[0:64, :])))

    def ssm_tile(self, t, is_s):
        P = self.P
        v3 = lambda b: b[:].rearrange("p (j n) -> p j n", j=4)
        if is_s:
            self.load("sp", self.h0r, self.h0r[:], self.h0re, self.h0re[:])
            self.load("sp", self.h0i, self.h0i[:], self.h0im, self.h0im[:])
        by = self.banks[6]

        def xmm(sg_):
            br_, bi_ = self.nb(), self.nb()
            for j in range(4):
                stt = 4 * sg_ + j
                self.mm(br_, br_[:, j * 128:(j + 1) * 128], self.KBT[0], self.KBT[0][:, stt, :], self.uTb, self.uTb[:, sg_, :])
                self.mm(bi_, bi_[:, j * 128:(j + 1) * 128], self.KBT[1], self.KBT[1][:, stt, :], self.uTb, self.uTb[:, sg_, :])
            return br_, bi_

        nxt = xmm(0)
        for sg in range(4):
            bxr, bxi = nxt
            if sg + 1 < 4:
                nxt = xmm(sg + 1)
            ss = slice(4 * sg, 4 * sg + 4)
            if is_s:
                self.cp("act", self.xhr, self.xhr[:], bxr, bxr[:])
                self.cp("act", self.xhi, self.xhi[:], bxi, bxi[:])
                S = [128, 4, NSEQ]
                arb = self.are[:, ss].unsqueeze(2).to_broadcast(S)
                aib = self.aim[:, ss].unsqueeze(2).to_broadcast(S)
                v4 = lambda b: b[:].rearrange("p (s q t) -> p s q t", s=4, t=DSEQ)
                q1, q2 = self.s1[:, 0:4, :], self.s2[:, 0:4, :]
                for tt_ in range(DSEQ):
                    pr_, pi_ = (self.h0r, self.h0i) if tt_ == 0 else (self.ghr, self.ghi)
                    pra = self.h0r[:, ss, :] if tt_ == 0 else v4(self.ghr)[:, :, :, tt_ - 1]
                    pia = self.h0i[:, ss, :] if tt_ == 0 else v4(self.ghi)[:, :, :, tt_ - 1]
                    self.tt("dve", self.s1, q1, pr_, pra, self.are, arb, ALU.mult)
                    self.tt("dve", self.s2, q2, pi_, pia, self.aim, aib, ALU.mult)
                    self.tt("dve", self.s1, q1, self.s1, q1, self.s2, q2, ALU.subtract)
                    self.tt("dve", self.s2, q2, pi_, pia, self.are, arb, ALU.mult)
                    self.tt("dve", self.tA, v4(self.tA)[:, :, :, tt_], self.s1, q1, self.xhr, v4(self.xhr)[:, :, :, tt_], ALU.add)
                    self.tt("dve", self.s1, q1, pr_, pra, self.aim, aib, ALU.mult)
                    self.tt("dve", self.s1, q1, self.s1, q1, self.s2, q2, ALU.add)
                    self.tt("dve", self.ghi, v4(self.ghi)[:, :, :, tt_], self.s1, q1, self.xhi, v4(self.xhi)[:, :, :, tt_], ALU.add)
                    self.cp("dve", self.ghr, v4(self.ghr)[:, :, :, tt_], self.tA, v4(self.tA)[:, :, :, tt_])
                self.cp("act", self.hreb, self.hreb[:], self.ghr, v3(self.ghr))
                self.cp("act", self.himb, self.himb[:], self.ghi, v3(self.ghi))
                self.cp("dve", self.hfr, self.hfr[:, ss, :], self.ghr, v4(self.ghr)[:, :, :, DSEQ - 1])
                self.cp("dve", self.hfi, self.hfi[:, ss, :], self.ghi, v4(self.ghi)[:, :, :, DSEQ - 1])
            else:
                tc, tsn = self.Tc[:, ss, :], self.Ts[:, ss, :]
                self.tt("dve", self.tA, v3(self.tA), bxr, v3(bxr), self.Tc, tc, ALU.mult)
                self.tt("dve", self.tB, v3(self.tB), bxi, v3(bxi), self.Ts, tsn, ALU.mult)
                self.tt("dve", self.ghr, v3(self.ghr), bxi, v3(bxi), self.Tc, tc, ALU.mult)
                self.tt("dve", self.ghi, v3(self.ghi), bxr, v3(bxr), self.Ts, tsn, ALU.mult)
                self.tt("dve", self.xhr, self.xhr[:], self.tA, self.tA[:], self.tB, self.tB[:], ALU.add)
                self.tt("dve", self.xhi, self.xhi[:], self.ghr, self.ghr[:], self.ghi, self.ghi[:], ALU.subtract)
                cr, ci = self.carr[:, ss], self.cari[:, ss]
                ct_, sn = self.cth[:, ss], self.sth[:, ss]
                self.tt("dve", self.c1, self.c1[:, ss], self.carr, cr, self.cth, ct_, ALU.mult)
                self.tt("dve", self.c2, self.c2[:, ss], self.cari, ci, self.sth, sn, ALU.mult)
                self.tt("dve", self.c3, self.c3[:, ss], self.carr, cr, self.sth, sn, ALU.mult)
                self.tt("dve", self.c4, self.c4[:, ss], self.cari, ci, self.cth, ct_, ALU.mult)
                self.tt("dve", self.gini_r, self.gini_r[:, ss], self.c1, self.c1[:, ss], self.c2, self.c2[:, ss], ALU.subtract)
                self.tt("dve", self.gini_i, self.gini_i[:, ss], self.c3, self.c3[:, ss], self.c4, self.c4[:, ss], ALU.add)
                for j in range(4):
                    stt = 4 * sg + j
                    mb = self.mag[:, stt:stt + 1].to_broadcast([128, 128])
                    for (dst, src, ini) in [(self.ghr, self.xhr, self.gini_r), (self.ghi, self.xhi, self.gini_i)]:
                        P.op("dve", lambda e, dst=dst, src=src, ini=ini, j=j, stt=stt, mb=mb: e.tensor_tensor_scan(
                            out=dst[:, j * 128:(j + 1) * 128], data0=mb, data1=src[:, j * 128:(j + 1) * 128],
                            initial=ini[:, stt:stt + 1], op0=ALU.mult, op1=ALU.add),
                            reads=[self.mag, src, ini], writes=[dst])
                pe_ = self.post_eng
                self.tt(pe_, self.pA, v3(self.pA), self.ghr, v3(self.ghr), self.Tc, tc, ALU.mult)
                self.tt(pe_, self.pB, v3(self.pB), self.ghi, v3(self.ghi), self.Ts, tsn, ALU.mult)
                self.tt(pe_, self.tD, v3(self.tD), self.ghi, v3(self.ghi), self.Tc, tc, ALU.mult)
                self.tt(pe_, self.xhr, v3(self.xhr), self.ghr, v3(self.ghr), self.Ts, tsn, ALU.mult)
                self.tt(pe_, self.tC, self.tC[:], self.pA, self.pA[:], self.pB, self.pB[:], ALU.subtract)
                self.tt(pe_, self.tD, self.tD[:], self.tD, self.tD[:], self.xhr, self.xhr[:], ALU.add)
                self.cp("act", self.hreb, self.hreb[:], self.tC, v3(self.tC))
                self.cp("act", self.himb, self.himb[:], self.tD, v3(self.tD))
                self.cp("act", self.carr, self.carr[:, ss], self.tC, v3(self.tC)[:, :, 127])
                self.cp("act", self.cari, self.cari[:, ss], self.tD, v3(self.tD)[:, :, 127])
            for j in range(4):
                stt = 4 * sg + j
                self.mm(by, by[:, sg * 128:(sg + 1) * 128], self.CT[0], self.CT[0][:, stt, :], self.hreb, self.hreb[:, j, :],
                        start=(j == 0), stop=False)
                self.mm(by, by[:, sg * 128:(sg + 1) * 128], self.CT[1], self.CT[1][:, stt, :], self.himb, self.himb[:, j, :],
                        start=False, stop=(j == 3))
        if is_s:
            self.store("sp", self.hre_s, self.hre_s[:], self.hfr, self.hfr[:])
            self.store("sp", self.him_s, self.him_s[:], self.hfi, self.hfi[:])
        elif t == NTILE - 2:
            self.store("sp", self.hre_p, self.hre_p[:], self.carr, self.carr[:])
            self.store("sp", self.him_p, self.him_p[:], self.cari, self.cari[:])
        db = self.dss[:].unsqueeze(2).to_broadcast([128, 4, 128])
        self.tt("dve", self.tA, v3(self.tA), self.uTb, self.uTb[:], self.dss, db, ALU.mult)
        self.tt("dve", self.tB, self.tB[:], self.tA, self.tA[:], by, by[:], ALU.add)
        self.act(self.ygb, self.ygb[:], self.tB, v3(self.tB), AF.Gelu)
        bz = self.nb()
        for co in range(4):
            for ci in range(4):
                self.mm(bz, bz[:, co * 128:(co + 1) * 128], self.wglub, self.wglub[:, ci, co * 128:(co + 1) * 128],
                        self.ygb, self.ygb[:, ci, :], start=(ci == 0), stop=(ci == 3))
        for co in range(4):
            self.act(self.sig, self.sig[:, co, :], bz, bz[:, co * 128:(co + 1) * 128], AF.Sigmoid,
                     bias=self.bglu[:, co:co + 1], extra_reads=[self.bglu])
        self.tt("dve", self.sTb, self.sTb[:], self.ygb, self.ygb[:], self.sig, self.sig[:], ALU.mult)
        if self.debug and t in (0, 1, NTILE - 1):
            self.dbg(f"dbg_aT_{t}", self.aTb, self.aTb[:], [64, 8, 128], BF16)
            self.dbg(f"dbg_sT_{t}", self.sTb, self.sTb[:], [128, 4, 128], BF16)

    def convert_tables(self):
        P = self.P
        self.uT_bf = [P.dram(f"uT_bf{r}", [D, 2048], BF16) for r in range(8)]
        self.v_bf = [P.dram(f"v_bf{r}", [2048, D], BF16) for r in range(8)]
        self.conv_done = 0

    def convert_some(self, n=1):
        P = self.P
        for _ in range(n):
            r = self.conv_done
            if r >= 8:
                return
            self.conv_done += 1
            P.dma("pool", lambda e, r=r: e.dma_start(out=self.uT_bf[r][:], in_=self.uT[:, 2048 * r:2048 * r + 2048]),
                  owner=self.uT_bf[r], reads=[self.uT], writes=[self.uT_bf[r]], nd=64)
            P.dma("pool", lambda e, r=r: e.dma_start(out=self.v_bf[r][:], in_=self.vtab[2048 * r:2048 * r + 2048, :]),
                  owner=self.v_bf[r], reads=[self.vtab], writes=[self.v_bf[r]], nd=128)

    def phase_b_run(self):
        P = self.P
        sb = P.sbuf
        NT = self.NT
        TW = 128 * NT
        self.convert_some(8)
        self.npool = 8 - 2 * NT
        self.bank_i = 0
        self.acc = [[self.banks[self.npool + 2 * i], self.banks[self.npool + 2 * i + 1]] for i in range(NT)]
        self.wqb = sb("wqb", [128, 8, 2048], BF16)
        self.load("pool", self.wqb, self.wqb[:], self.w_q, self.w_q[:].rearrange("(k p) n -> p k n", p=128))
        self.keysb = sb("keysb", [128, 16, 128], BF16)
        self.load("pool", self.keysb, self.keysb[:], self.keysT, self.keysT[:])
        self.wgb = sb("wgb", [128, 8, D], BF16)
        self.load("pool", self.wgb, self.wgb[:], self.w_gate, self.w_gate[:].rearrange("(k p) n -> p k n", p=128))
        self.wppb = sb("wppb", [128, 2, D], BF16)
        self.load("pool", self.wppb, self.wppb[:], self.w_pp, self.w_pp[:].rearrange("(k p) n -> p k n", p=128))
        self.g2 = sb("g2", [128, D], F32)
        self.b2 = sb("b2", [128, D], F32)
        self.load("sp", self.g2, self.g2[:], self.ln2g, self.ln2g[:])
        self.load("sp", self.b2, self.b2[:], self.ln2b, self.ln2b[:])
        self.iotab = sb("iotab", [128, 128], BF16)
        self.load("pool", self.iotab, self.iotab[:], self.iota128, self.iota128[:])
        self.io16 = sb("io16", [128, 16], F32)
        self.load("sp", self.io16, self.io16[:], self.iota16, self.iota16[:])
        EC = self.EC
        self.ut = [sb(f"ut{i}", [128, 8, EC * 128], BF16) for i in range(self.NBUF)]
        self.vt = [sb(f"vt{i}", [128, EC, D], BF16) for i in range(self.NBUF)]
        self.Gall = sb("Gall", [128, TW, 128], BF16)
        self.OH = [sb(f"OH{i}", [128, 1024], F32) for i in range(4)]
        self.x1f = [sb(f"x1f{i}", [128, D], F32) for i in range(1)]
        self.x1b = sb("x1b", [128, D], BF16)
        self.x1T = [sb(f"x1T{i}", [128, 8, TW], BF16) for i in range(2)]
        self.qsB = sb("qsB", [128, D], F32)
        self.candB = sb("candB", [128, 512], F32)
        self.tmp128 = sb("tmp128", [128, 128], F32)
        self.v16 = sb("v16", [128, 16, 16], F32)
        self.i16 = sb("i16", [128, 16, 16], U32)
        self.i16f = sb("i16f", [128, 16, 16], F32)
        self.sc16 = sb("sc16", [128, 8, 16], F32)
        self.ci = sb("ci", [128, 8, 16], U32)
        self.ia = sb("ia", [128, 8, 16], U32)
        self.ib = sb("ib", [128, 8, 16], U32)
        self.iaf = sb("iaf", [128, 8, 16], F32)
        self.ibf = sb("ibf", [128, 8, 16], F32)
        self.e12g = sb("e12g", [128, 3, 128], F32)
        self.gsum = sb("gsum", [128, 8], F32)
        self.egT = [sb(f"egT{i}", [128, 3, TW], BF16) for i in range(2)]
        self.nb2 = [sb(f"nb2{i}", [128, TW], F32) for i in range(2)]
        self.Hg = [sb(f"Hg{i}", [128, TW], BF16) for i in range(self.NB)]
        self.Hm = [sb(f"Hm{i}", [128, TW], BF16) for i in range(self.NB)]
        self.pTb = sb("pTb", [128, 2, 128], BF16)
        self.stats = sb("statsB", [128, 2, 6], F32)
        self.mv = sb("mvB", [128, 2], F32)
        self.rstd = sb("rstdB", [128, 1], F32)
        try:
            print("phase B sbuf bytes remaining:", self.nc.sbuf_bytes_remaining)
        except Exception as ex:
            print("sbuf_bytes_remaining n/a", ex)
        tl = self.tiles
        sts = [tl[i:i + NT] for i in range(0, len(tl), NT)]
        sts.sort(key=lambda g: (len(g) == NT))

        def drain(g):
            for _ in g:
                pass

        def front(si):
            for idx, t in enumerate(sts[si]):
                yield from self.retrieval(t, idx, si % 2)

        def gcon(si):
            for idx, t in enumerate(sts[si]):
                yield from self.gconstruct(t, idx, si % 2)

        def epi(si):
            for idx, t in enumerate(sts[si]):
                yield from self.epilogue_b(t, idx)

        def interleave(ga, gb):
            ga, gb = iter(ga), iter(gb)
            da = db = False
            while not (da and db):
                if not da:
                    try:
                        next(ga)
                    except StopIteration:
                        da = True
                if not db:
                    try:
                        next(gb)
                    except StopIteration:
                        db = True

        drain(front(0))
        drain(gcon(0))
        for si in range(len(sts)):
            bg = front(si + 1) if si + 1 < len(sts) else iter(())
            self.dense(sts[si], si % 2, bg)
            drain(bg)
            if si + 1 < len(sts):
                interleave(epi(si), gcon(si + 1))
            else:
                drain(epi(si))

    def transpose_to(self, srcb, dstT, col0, dstap=None, eng="act"):
        P = self.P
        bk = self.nb()
        bkb = bk[:].bitcast(BF16)
        for kt in range(8):
            P.op("pe", lambda e, kt=kt, bkb=bkb: e.transpose(out=bkb[:, kt * 128:(kt + 1) * 128],
                                                             in_=srcb[:, kt * 128:(kt + 1) * 128], identity=self.identb[:]),
                 reads=[srcb, self.identb], writes=[bk], pe_accum=True)
        dap = dstT[:, :, col0:col0 + 128] if dstap is None else dstap
        self.cp(eng, dstT, dap, bk, bkb.rearrange("p (k n) -> p k n", k=8))

    def retrieval(self, t, idx, par):
        P = self.P
        x1f = self.x1f[0]
        x1T, egT = self.x1T[par], self.egT[par]
        qTB = self.qsB
        qT = self.qsB[:].bitcast(BF16).rearrange("p (i n) -> p i n", i=16)
        candB = self.candB
        cand, cand2 = self.candB[:, 0:256], self.candB[:, 256:512]
        OH = self.OH
        scb = lambda i: OH[i // 8]
        sca = lambda i: OH[i // 8][:, (i % 8) * 128:(i % 8 + 1) * 128]
        tmb = lambda i: OH[2 + i // 8]
        tma = lambda i: OH[2 + i // 8][:, (i % 8) * 128:(i % 8 + 1) * 128]
        self.load("act", x1f, x1f[:], self.X1[t], self.X1[t][:])
        yield
        yield
        self.cp("dve", self.x1b, self.x1b[:], x1f, x1f[:])
        yield
        self.transpose_to(self.x1b, x1T, idx * 128, eng="dve")
        yield
        xc = slice(idx * 128, idx * 128 + 128)
        v3 = lambda b: b[:].rearrange("p (j n) -> p j n", j=4)
        for grp in range(4):
            bq = self.nb()
            for j in range(4):
                i = 4 * grp + j
                for kt in range(8):
                    self.mm(bq, bq[:, j * 128:(j + 1) * 128], self.wqb, self.wqb[:, kt, i * 128:(i + 1) * 128],
                            x1T, x1T[:, kt, xc], start=(kt == 0), stop=(kt == 7))
            yield
            self.cp("dve", qTB, qT[:, 4 * grp:4 * grp + 4, :], bq, v3(bq))
            yield
        for grp in range(4):
            bs = self.nb()
            for j in range(4):
                i = 4 * grp + j
                self.mm(bs, bs[:, j * 128:(j + 1) * 128], qTB, qT[:, i, :], self.keysb, self.keysb[:, i, :])
            yield
            self.cp("dve", OH[grp // 2], OH[grp // 2][:, (grp % 2) * 512:(grp % 2) * 512 + 512], bs, bs[:])
            yield

        def top16(srcB, src, tmpB, tmp, valB, val, idxB, idxv):
            P.op("dve", lambda e: e.max(out=val[:, 0:8], in_=src), reads=[srcB], writes=[valB])
            P.op("dve", lambda e: e.match_replace(out=tmp, in_to_replace=val[:, 0:8], in_values=src, imm_value=-1e30),
                 reads=[srcB, valB], writes=[tmpB])
            P.op("dve", lambda e: e.max(out=val[:, 8:16], in_=tmp), reads=[tmpB], writes=[valB])
            P.op("dve", lambda e: e.max_index(out=idxv[:, 0:8], in_max=val[:, 0:8], in_values=src), reads=[srcB, valB], writes=[idxB])
            P.op("dve", lambda e: e.max_index(out=idxv[:, 8:16], in_max=val[:, 8:16], in_values=tmp), reads=[tmpB, valB], writes=[idxB])

        for i in range(16):
            P.op("dve", lambda e, i=i: e.max(out=self.v16[:, i, 0:8], in_=sca(i)), reads=[scb(i)], writes=[self.v16])
            if i % 4 == 3:
                yield
        for i in range(16):
            P.op("dve", lambda e, i=i: e.match_replace(out=tma(i), in_to_replace=self.v16[:, i, 0:8], in_values=sca(i),
                                                       imm_value=-1e30), reads=[scb(i), self.v16], writes=[tmb(i)])
            if i % 4 == 3:
                yield
        for i in range(16):
            P.op("dve", lambda e, i=i: e.max(out=self.v16[:, i, 8:16], in_=tma(i)), reads=[tmb(i)], writes=[self.v16])
            if i % 4 == 3:
                yield
        for i in range(16):
            P.op("dve", lambda e, i=i: e.max_index(out=self.i16[:, i, 0:8], in_max=self.v16[:, i, 0:8], in_values=sca(i)),
                 reads=[scb(i), self.v16], writes=[self.i16])
            P.op("dve", lambda e, i=i: e.max_index(out=self.i16[:, i, 8:16], in_max=self.v16[:, i, 8:16], in_values=tma(i)),
                 reads=[tmb(i), self.v16], writes=[self.i16])
            if i % 2 == 1:
                yield
        self.cp("dve", self.i16f, self.i16f[:], self.i16, self.i16[:])
        c3 = cand.rearrange("p (a b) -> p a b", a=16)
        for h in range(8):
            self.tt("dve", candB, c3, self.v16, self.v16[:, 2 * h, :].unsqueeze(2).to_broadcast([128, 16, 16]),
                    self.v16, self.v16[:, 2 * h + 1, :].unsqueeze(1).to_broadcast([128, 16, 16]), ALU.add)
            val, idxv = self.sc16[:, h, :], self.ci[:, h, :]
            P.op("dve", lambda e, val=val: e.max(out=val[:, 0:8], in_=cand), reads=[candB], writes=[self.sc16])
            P.op("dve", lambda e, val=val: e.match_replace(out=cand2, in_to_replace=val[:, 0:8], in_values=cand, imm_value=-1e30),
                 reads=[candB, self.sc16], writes=[candB])
            yield
            P.op("dve", lambda e, val=val: e.max(out=val[:, 8:16], in_=cand2), reads=[candB], writes=[self.sc16])
            P.op("dve", lambda e, val=val, idxv=idxv: e.max_index(out=idxv[:, 0:8], in_max=val[:, 0:8], in_values=cand),
                 reads=[candB, self.sc16], writes=[self.ci])
            yield
            P.op("dve", lambda e, val=val, idxv=idxv: e.max_index(out=idxv[:, 8:16], in_max=val[:, 8:16], in_values=cand2),
                 reads=[candB, self.sc16], writes=[self.ci])
            yield
        P.op("dve", lambda e: e.tensor_single_scalar(out=self.ia[:], in_=self.ci[:], scalar=4, op=ALU.logical_shift_right),
             reads=[self.ci], writes=[self.ia])
        P.op("dve", lambda e: e.tensor_single_scalar(out=self.ib[:], in_=self.ci[:], scalar=15, op=ALU.bitwise_and),
             reads=[self.ci], writes=[self.ib])
        self.cp("dve", self.iaf, self.iaf[:], self.ia, self.ia[:])
        self.cp("dve", self.ibf, self.ibf[:], self.ib, self.ib[:])
        yield
        S4 = [128, 4, 16, 16]
        iob = self.io16[:].unsqueeze(1).unsqueeze(1).to_broadcast(S4)
        i4 = self.i16f[:].rearrange("p (h two) a -> p h two a", two=2)
        for which, srcf in enumerate([self.iaf, self.ibf]):
            for hf in range(2):
                hs = slice(4 * hf, 4 * hf + 4)
                ohB = OH[hf]
                oh = OH[hf][:].rearrange("p (h k a) -> p h k a", h=4, k=16)
                self.tt("dve", ohB, oh, srcf, srcf[:, hs, :].unsqueeze(3).to_broadcast(S4), self.io16, iob, ALU.is_equal)
                yield
                self.tt("dve", ohB, oh, ohB, oh, self.i16f, i4[:, hs, which, :].unsqueeze(2).to_broadcast(S4), ALU.mult)
                yield
                P.op("dve", lambda e, which=which, hf=hf, oh=oh: e.tensor_reduce(
                    out=self.e12g[:, which, 64 * hf:64 * hf + 64].rearrange("p (h k) -> p h k", h=4), in_=oh, axis=AX.X, op=ALU.add),
                    reads=[ohB], writes=[self.e12g])
                yield
        g3 = self.e12g[:, 2, :].rearrange("p (h k) -> p h k", h=8)
        self.tt("dve", self.e12g, g3, self.sc16, self.sc16[:], self.sc16, self.sc16[:, :, 0:1].to_broadcast([128, 8, 16]), ALU.subtract)
        for _ in range(10):
            yield
        self.act(self.e12g, g3, self.e12g, g3, AF.Exp)
        yield
        yield
        P.op("dve", lambda e: e.tensor_reduce(out=self.gsum[:], in_=g3, axis=AX.X, op=ALU.add), reads=[self.e12g], writes=[self.gsum])
        P.op("dve", lambda e: e.reciprocal(out=self.gsum[:], in_=self.gsum[:]), reads=[self.gsum], writes=[self.gsum])
        self.tt("dve", self.e12g, g3, self.e12g, g3, self.gsum, self.gsum[:].unsqueeze(2).to_broadcast([128, 8, 16]), ALU.mult)
        yield
        yield
        bt = self.nb()
        for w_ in range(3):
            P.op("pe", lambda e, w_=w_: e.transpose(out=bt[:, w_ * 128:(w_ + 1) * 128], in_=self.e12g[:, w_, :], identity=self.identf[:]),
                 reads=[self.e12g, self.identf], writes=[bt], pe_accum=True)
        yield
        self.cp("dve", egT, egT[:, :, xc], bt, bt[:, 0:384].rearrange("p (w n) -> p w n", w=3))
        yield

    def gconstruct(self, t, idx, par):
        P = self.P
        egT = self.egT[par]
        nb2 = self.nb2[par]
        v3 = lambda b: b[:].rearrange("p (j n) -> p j n", j=4)
        S3 = [128, 16, 128]
        iob3 = self.iotab[:].unsqueeze(1).to_broadcast(S3)
        for st in range(8):
            A1B, BqB = self.OH[st % 2], self.OH[2 + st % 2]
            A1q = A1B[:].bitcast(BF16).rearrange("p (t n) -> p t n", t=16)
            Bq = BqB[:].bitcast(BF16).rearrange("p (t n) -> p t n", t=16)
            qs = slice(idx * 128 + 16 * st, idx * 128 + 16 * st + 16)
            for tq in range(16):
                tok = idx * 128 + 16 * st + tq
                P.op("dve", lambda e, tq=tq, tok=tok, A1q=A1q: e.tensor_scalar(
                    out=A1q[:, tq, :], in0=self.iotab[:], scalar1=egT[:, 0, tok:tok + 1], scalar2=egT[:, 2, tok:tok + 1],
                    op0=ALU.is_equal, op1=ALU.mult), reads=[self.iotab, egT], writes=[A1B])
                beng = "pool" if (tq % 8) < self.BPOOL else "dve"
                P.op(beng, lambda e, tq=tq, tok=tok, Bq=Bq: e.tensor_scalar(
                    out=Bq[:, tq, :], in0=self.iotab[:], scalar1=egT[:, 1, tok:tok + 1], scalar2=None,
                    op0=ALU.is_equal), reads=[self.iotab, egT], writes=[BqB])
            for b4 in range(4):
                bg = self.nb()
                for j in range(4):
                    tq = 4 * b4 + j
                    self.mm(bg, bg[:, j * 128:(j + 1) * 128], BqB, Bq[:, tq, :], A1B, A1q[:, tq, :])
                tok0 = idx * 128 + 16 * st + 4 * b4
                self.cp("act", self.Gall, self.Gall[:, tok0:tok0 + 4, :], bg, v3(bg))
            yield

    def dense(self, tl, par, bg):
        P = self.P
        NT, EC = len(tl), self.EC
        TW = 128 * NT
        x1T = self.x1T[par]
        nchunk = 128 // EC
        NB, NBUF = self.NB, self.NBUF

        def stage1(j):
            c, jj = divmod(j, EC)
            ut, vt = self.ut[c % NBUF], self.vt[c % NBUF]
            if jj == 0:
                r, off = (c * EC) // 16, ((c * EC) % 16) * 128
                self.load("sp", ut, ut[:], self.uT_bf[r], self.uT_bf[r][:, off:off + EC * 128].rearrange("(k p) e -> p k e", p=128))
                self.load("sp", vt, vt[:], self.v_bf[r], self.v_bf[r][off:off + EC * 128, :].rearrange("(j p) d -> p j d", p=128))
            bh = self.nb()
            Hg, Hm = self.Hg[j % NB], self.Hm[j % NB]
            for kt in range(8):
                self.mm(bh, bh[:, 0:TW], ut, ut[:, kt, jj * 128:(jj + 1) * 128], x1T, x1T[:, kt, 0:TW],
                        start=(kt == 0), stop=(kt == 7))
            self.act(Hg, Hg[:, 0:TW], bh, bh[:, 0:TW], AF.Gelu)
            self.tt(self.mask_eng, Hm, Hm[:, 0:TW], Hg, Hg[:, 0:TW], self.Gall, self.Gall[:, 0:TW, j], ALU.mult)

        def stage2(j):
            c, jj = divmod(j, EC)
            vt = self.vt[c % NBUF]
            Hm = self.Hm[j % NB]
            for idx in range(NT):
                for hh in range(2):
                    ab = self.acc[idx][hh]
                    self.mm(ab, ab[:], Hm, Hm[:, idx * 128:(idx + 1) * 128], vt, vt[:, jj, 512 * hh:512 * hh + 512],
                            start=(j == 0), stop=(j == 127))

        SK = NB - 1
        self._bgacc = 0.0
        for j in range(min(SK, 128)):
            stage1(j)
        for j in range(128):
            if j + SK < 128:
                stage1(j + SK)
            stage2(j)
            self._bgacc += self.BGR
            while self._bgacc >= 1.0:
                self._bgacc -= 1.0
                next(bg, None)

    def epilogue_b(self, t, idx):
        P = self.P
        c0 = 128 * t
        sgB, sgt = self.qsB, self.qsB[:]
        x2TB = self.candB
        x2T = self.candB[:].bitcast(BF16).rearrange("p (k n) -> p k n", k=8)
        x = self.x1f[0]
        if self.NT > 1 or True:
            self.load("sp", x, x[:], self.X1[t], self.X1[t][:])
        for hh in range(2):
            ab = self.acc[idx][hh]
            P.op("dve", lambda e, hh=hh, ab=ab: e.scalar_tensor_tensor(
                out=x[:, 512 * hh:512 * hh + 512], in0=x[:, 512 * hh:512 * hh + 512], scalar=ALPHA,
                in1=ab[:], op0=ALU.mult, op1=ALU.add), reads=[x, ab], writes=[x])
        if self.debug and t in (0, 1, NTILE - 1):
            self.dbg(f"dbg_r2_{t}", x, x[:], [128, D])
        yield
        self.layer_norm(x, x, self.g2, self.b2, part=1)
        yield
        self.layer_norm_fin(x, x, self.g2, self.b2)
        yield
        self.cp("act", self.x1b, self.x1b[:], x, x[:])
        yield
        self.transpose_to(self.x1b, x2TB, 0, dstap=x2T)
        yield
        self.load("pool", self.pTb, self.pTb[:], self.pT, self.pT[:, c0:c0 + 128].rearrange("(k p) n -> p k n", p=128))
        for hh in range(2):
            bgt, bpp = self.acc[idx][0], self.acc[idx][1]
            for kt in range(8):
                self.mm(bgt, bgt[:], x2TB, x2T[:, kt, :], self.wgb, self.wgb[:, kt, 512 * hh:512 * hh + 512],
                        start=(kt == 0), stop=(kt == 7))
            for k2 in range(2):
                self.mm(bpp, bpp[:], self.pTb, self.pTb[:, k2, :], self.wppb, self.wppb[:, k2, 512 * hh:512 * hh + 512],
                        start=(k2 == 0), stop=(k2 == 1))
            sl = slice(512 * hh, 512 * hh + 512)
            self.act(sgB, sgt[:, sl], bgt, bgt[:], AF.Sigmoid)
            yield
            self.tt("dve", sgB, sgt[:, sl], sgB, sgt[:, sl], bpp, bpp[:], ALU.mult)
            self.tt("dve", sgB, sgt[:, sl], sgB, sgt[:, sl], x, x[:, sl], ALU.add)
            yield
        self.store("sp", self.y, self.y[c0:c0 + 128, :], sgB, sgt)


def _rope_tables():
    half = 32
    inv = (10000.0 ** (-np.arange(half, dtype=np.float32) / np.float32(half))).astype(np.float32)
    pos = np.concatenate([np.arange(TP, dtype=np.int64), 16384 + (np.arange(TS) % DSEQ)]).astype(np.float32)
    ang = (pos[:, None] * inv[None, :]).astype(np.float32)
    cos = np.cos(ang).astype(np.float32)
    sin = np.sin(ang).astype(np.float32)
    r = np.arange(128)
    dd = r % 64
    j = dd % 32
    sgn = np.where(dd < 32, -1.0, 1.0).astype(np.float32)
    C = np.ascontiguousarray(cos[:, j].T)
    S = np.ascontiguousarray((sin[:, j] * sgn[None, :]).T)
    rows = np.concatenate([np.arange(TP - 128, TP), np.arange(TP, TT)])
    return C, S, np.ascontiguousarray(cos[rows]), np.ascontiguousarray(sin[rows])


def _masks():
    k = np.arange(128)[:, None]
    q = np.arange(128)[None, :]
    own = np.where(q >= k, 0.0, NEG).astype(np.float32)
    prev = np.where(k > q, 0.0, NEG).astype(np.float32)
    same = (k // DSEQ) == (q // DSEQ)
    new = np.where(same & ((k % DSEQ) <= (q % DSEQ)), 0.0, NEG).astype(np.float32)
    qi = np.arange(DSEQ)[None, :]
    past = np.where(k > qi, 0.0, NEG).astype(np.float32)
    t4 = lambda m: np.ascontiguousarray(np.tile(m, (1, 4)))
    return t4(own), t4(prev), t4(new), t4(past)


def _st_layout(a):
    a = np.asarray(a, np.float32)
    rest = a.shape[2:]
    return np.ascontiguousarray(np.moveaxis(a.reshape((16, 128) + rest), 0, 1))


def _prep_shared(inp):
    w = {}
    w_in = np.asarray(inp["w_in"][0], np.float32)
    q = w_in[:, 0:512].reshape(D, 8, 64)
    k = w_in[:, 512:640].reshape(D, 2, 64)
    partner = lambda a: np.concatenate([a[..., 32:], a[..., :32]], axis=-1)
    qt = lambda a: np.concatenate([np.concatenate([a[:, i], a[:, i + 4]], axis=1) for i in range(4)], axis=1)
    cols = [qt(q), qt(partner(q)), k.reshape(D, 128), partner(k).reshape(D, 128), w_in[:, 768:1280],
            w_in[:, 1280:2304], w_in[:, 2304:3328], w_in[:, 640:768]]
    w["w_in"] = np.ascontiguousarray(np.concatenate(cols, axis=1))
    C, S, Ct, St = _rope_tables()
    w["ropeC"], w["ropeS"], w["ropeCt"], w["ropeSt"] = C, S, Ct, St
    w["sinks"] = np.ascontiguousarray(np.tile(np.asarray(inp["attn_sinks"][0], np.float32)[None, :], (128, 1)))
    w["w_ap"] = np.ascontiguousarray(inp["w_attn_proj"][0])
    w["w_sp"] = np.ascontiguousarray(inp["w_ssm_proj"][0])
    w["w_glu"] = np.ascontiguousarray(inp["w_glu"][0])
    w["b_glu"] = np.ascontiguousarray(np.asarray(inp["b_glu"][0], np.float32).reshape(4, 128).T)
    w["w_out"] = np.ascontiguousarray(inp["w_out"][0])
    rep = lambda v: np.ascontiguousarray(np.tile(np.asarray(v, np.float32)[None, :], (128, 1)))
    w["ln1g"], w["ln1b"] = rep(inp["ln1_g"][0]), rep(inp["ln1_b"][0])
    w["ln2g"], w["ln2b"] = rep(inp["ln2_g"][0]), rep(inp["ln2_b"][0])
    w["lam_re"] = _st_layout(inp["ssm_lambda_re"][0])
    w["lam_im"] = _st_layout(inp["ssm_lambda_im"][0])
    w["logdt"] = _st_layout(np.repeat(np.asarray(inp["ssm_log_dt"][0], np.float32)[:, None], 64, axis=1))
    w["Bre"] = _st_layout(inp["ssm_b_re"][0])
    w["Bim"] = _st_layout(inp["ssm_b_im"][0])
    w["Cre"] = _st_layout(np.swapaxes(np.asarray(inp["ssm_c_re"][0]), 1, 2))
    w["Cim"] = _st_layout(np.swapaxes(np.asarray(inp["ssm_c_im"][0]), 1, 2))
    w["dssm"] = np.ascontiguousarray(np.asarray(inp["ssm_d"][0], np.float32).reshape(4, 128).T)
    mo, mp, mn, mpa = _masks()
    w["mask_own"], w["mask_prev"], w["mask_new"], w["mask_past"] = mo, mp, mn, mpa
    w["iota_tau"] = np.ascontiguousarray(np.tile(np.arange(128, dtype=np.float32)[None, :], (128, 1)))
    w["iota128"] = w["iota_tau"]
    w["iota16"] = np.ascontiguousarray(np.tile(np.arange(16, dtype=np.float32)[None, :], (128, 1)))
    w["w_q"] = np.ascontiguousarray(inp["peer_w_q"][0])
    k1 = np.asarray(inp["peer_keys1"][0], np.float32)
    k2 = np.asarray(inp["peer_keys2"][0], np.float32)
    kk = np.stack([k1, k2], axis=1).reshape(16, 128, 128)
    w["keysT"] = np.ascontiguousarray(np.transpose(kk, (2, 0, 1)))
    w["uT"] = np.ascontiguousarray(np.asarray(inp["peer_u"][0], np.float32).T)
    w["vtab"] = np.ascontiguousarray(inp["peer_v"][0])
    w["w_gate"] = np.ascontiguousarray(inp["ple_w_gate"][0])
    w["w_pp"] = np.ascontiguousarray(inp["ple_w_proj"][0])
    return w


def _prep_core(inp, c):
    m = {}
    xs = np.asarray(inp["x_sample"][16 * c:16 * c + 16], np.float32).reshape(TS, D)
    x = np.concatenate([np.asarray(inp["x_prompt"][c], np.float32), xs], axis=0)
    m["x"] = np.ascontiguousarray(x)
    m["xT"] = np.ascontiguousarray(x.T)
    ps = np.asarray(inp["p_sample"][0, 16 * c:16 * c + 16], np.float32).reshape(TS, 256)
    p = np.concatenate([np.asarray(inp["p_prompt"][0, c], np.float32), ps], axis=0)
    m["pT"] = np.ascontiguousarray(p.T)
    wk = np.asarray(inp["state_win_k"][0, 16 * c:16 * c + 16], np.float32)
    wv = np.asarray(inp["state_win_v"][0, 16 * c:16 * c + 16], np.float32)
    m["wkT"] = np.ascontiguousarray(np.transpose(wk, (2, 3, 0, 1)).reshape(128, NSEQ, 128))
    m["wv"] = np.ascontiguousarray(np.transpose(wv, (1, 0, 2, 3)).reshape(128, NSEQ, 128))
    m["wk_raw"] = np.ascontiguousarray(wk.reshape(NSEQ, 128, 128))
    m["wv_raw"] = np.ascontiguousarray(wv.reshape(NSEQ, 128, 128))
    hr = np.asarray(inp["state_ssm_re"][0, 16 * c:16 * c + 16], np.float32)
    hi = np.asarray(inp["state_ssm_im"][0, 16 * c:16 * c + 16], np.float32)
    m["h0re"] = _st_layout(np.moveaxis(hr, 0, 2))
    m["h0im"] = _st_layout(np.moveaxis(hi, 0, 2))
    return m


def _from_st(a):
    rest = a.shape[2:]
    return np.moveaxis(a, 0, 1).reshape((32, 64) + rest)


_CACHE = {}


def kernel(**inputs):
    if "nc" not in _CACHE:
        _CACHE["nc"] = KB(NT=2).build()
    nc = _CACHE["nc"]
    shared = _prep_shared(inputs)
    in_maps = []
    for c in range(NCORES):
        m = dict(shared)
        m.update(_prep_core(inputs, c))
        in_maps.append(m)
    res = run_bass_kernel_spmd(nc, in_maps, core_ids=list(range(NCORES)))
    R = res.results
    y_p = np.stack([R[c]["y"][:TP] for c in range(NCORES)], 0).astype(np.float32)
    y_s = np.concatenate([R[c]["y"][TP:].reshape(NSEQ, DSEQ, D) for c in range(NCORES)], 0).astype(np.float32)
    nk_p = np.stack([R[c]["nk_p"].reshape(128, 2, 64) for c in range(NCORES)], 0)[None].astype(np.float32)
    nv_p = np.stack([R[c]["nv_p"].reshape(128, 2, 64) for c in range(NCORES)], 0)[None].astype(np.float32)
    hre_p = np.stack([_from_st(R[c]["hre_p"]) for c in range(NCORES)], 0)[None].astype(np.float32)
    him_p = np.stack([_from_st(R[c]["him_p"]) for c in range(NCORES)], 0)[None].astype(np.float32)
    nk_s = np.concatenate([R[c]["nk_s"].reshape(NSEQ, 128, 2, 64) for c in range(NCORES)], 0)[None].astype(np.float32)
    nv_s = np.concatenate([R[c]["nv_s"].reshape(NSEQ, 128, 2, 64) for c in range(NCORES)], 0)[None].astype(np.float32)
    hre_s = np.concatenate([np.moveaxis(_from_st(R[c]["hre_s"]), 2, 0) for c in range(NCORES)], 0)[None].astype(np.float32)
    him_s = np.concatenate([np.moveaxis(_from_st(R[c]["him_s"]), 2, 0) for c in range(NCORES)], 0)[None].astype(np.float32)
    return (y_p, y_s, nk_p, nv_p, hre_p, him_p, nk_s, nv_s, hre_s, him_s)
```

```python
from contextlib import ExitStack
import numpy as np
import concourse.bass as bass
import concourse.mybir as mybir
from concourse.bass_utils import run_bass_kernel_spmd

F32 = mybir.dt.float32
BF16 = mybir.dt.bfloat16
I32 = mybir.dt.int32
U32 = mybir.dt.uint32
AF = mybir.ActivationFunctionType
ALU = mybir.AluOpType
AX = mybir.AxisListType

NCORES = 8
D = 1024
TP = 4096
TS = 128
TT = TP + TS
NTILE = TT // 128
NSEQ = 16
DSEQ = 8
ALPHA = 2.0 ** 0.25
LN_EPS = 1e-5
NEG = -30000.0
QO, QP, KO, KP, UO, GA, GB, VO, WIN = 0, 512, 1024, 1152, 1280, 1792, 2816, 3840, 3968
TWO_PI = float(2.0 * np.pi)
PI = float(np.pi)

ENGS = ["pe", "act", "dve", "pool", "sp"]


class Buf:
    def __init__(self, name, t):
        self.name = name
        self.t = t
        self.w = None
        self.r = []
        self.dsem = None
        self.dcount = 0

    def __getitem__(self, idx):
        return self.t[idx]


class Prog:
    def __init__(self, nc):
        self.nc = nc
        self.q = {e: [] for e in ENGS}
        self.cnt = {e: 0 for e in ENGS}
        self.semkeys = [f"e_{e}" for e in ENGS]
        self.waited = {e: {} for e in ENGS}
        self.ndsem = 0
        self.stack = None
        self.final = []
        self.dma_last = {}
        self.pool_out = []

    def sbuf(self, name, shape, dt):
        return Buf(name, self.stack.enter_context(self.nc.sbuf_tensor(name, list(shape), dt)))

    def psum(self, name, shape, dt):
        return Buf(name, self.stack.enter_context(self.nc.psum_tensor(name, list(shape), dt)))

    def dram(self, name, shape, dt, kind="Internal"):
        return Buf(name, self.nc.dram_tensor(name, list(shape), dt, kind=kind))

    def _need(self, eng, tok, same_ok=False):
        if tok is None:
            return
        key, val = tok
        if same_ok and key == f"e_{eng}":
            return
        if self.waited[eng].get(key, 0) >= val:
            return
        self.waited[eng][key] = val
        self.q[eng].append(("wait", key, val))

    def _deps(self, eng, reads, writes, pe_accum=False):
        same_ok = pe_accum or eng in ("dve", "act")
        for b in reads:
            self._need(eng, b.w)
        for b in writes:
            self._need(eng, b.w, same_ok=same_ok)
            for tk in b.r:
                self._need(eng, tk, same_ok=same_ok)

    def _mark(self, tok, reads, writes):
        for b in reads:
            if b not in writes:
                b.r.append(tok)
                if len(b.r) > 16:
                    m = {}
                    for k, v in b.r:
                        m[k] = max(m.get(k, 0), v)
                    b.r = list(m.items())
        for b in writes:
            b.w = tok
            b.r = []

    def op(self, eng, fn, reads=(), writes=(), pe_accum=False):
        self._deps(eng, reads, writes, pe_accum)
        self.cnt[eng] += 1
        tok = (f"e_{eng}", self.cnt[eng])
        self.q[eng].append(("op", fn, tok[0], 1))
        self._mark(tok, reads, writes)
        return tok

    def dma(self, eng, fn, owner, reads=(), writes=(), final=False, nd=64):
        if eng == "pool":
            while self.pool_out and sum(n for _, n in self.pool_out) + nd > 700:
                tk, _ = self.pool_out.pop(0)
                self._need("pool", tk)
        self._deps(eng, reads, writes)
        if owner.dsem is None:
            owner.dsem = f"d{self.ndsem}_{owner.name}"
            self.ndsem += 1
            self.semkeys.append(owner.dsem)
        owner.dcount += 16
        tok = (owner.dsem, owner.dcount)
        self.dma_last[owner.dsem] = owner.dcount
        self.q[eng].append(("op", fn, tok[0], 16))
        if eng == "pool":
            self.pool_out.append((tok, nd))
        self._mark(tok, reads, writes)
        if final:
            self.final.append(tok)
        return tok

    def barrier(self):
        toks = [(f"e_{e}", self.cnt[e]) for e in ENGS if self.cnt[e] > 0]
        toks += [(k, v) for k, v in self.dma_last.items()]
        for e in ENGS:
            for tk in toks:
                self._need(e, tk)

    def emit(self, stack):
        nc = self.nc
        for tok in self.final:
            self._need("sp", tok)
        needed = {f"e_{e}": set() for e in ENGS}
        for e in ENGS:
            for it in self.q[e]:
                if it[0] == "wait" and it[1] in needed:
                    needed[it[1]].add(it[2])
        rank = {k: {v: i + 1 for i, v in enumerate(sorted(vs))} for k, vs in needed.items()}
        sems = {k: stack.enter_context(nc.semaphore(k)) for k in self.semkeys}
        block = stack.enter_context(nc.Block())

        def run(engname):
            ekey = f"e_{engname}"

            def body(e):
                n = 0
                for it in self.q[engname]:
                    if it[0] == "wait":
                        val = rank[it[1]][it[2]] if it[1] in rank else it[2]
                        e.wait_ge(sems[it[1]], val)
                    else:
                        ins = it[1](e)
                        if it[2] == ekey:
                            n += 1
                            if n in needed[ekey]:
                                ins.then_inc(sems[ekey], 1)
                        else:
                            ins.then_inc(sems[it[2]], it[3])
            return body

        block.tensor(run("pe"))
        block.scalar(run("act"))
        block.vector(run("dve"))
        block.gpsimd(run("pool"))
        block.sync(run("sp"))


class KB:
    def __init__(self, tiles=None, phase_b=True, debug=False, NT=1, EC=2, phase_a=True, NB=3, NBUF=3, BGR=1.6, mask_eng="pool", post_eng="dve", BPOOL=0):
        self.NT, self.EC, self.do_a, self.NB, self.NBUF, self.BGR, self.mask_eng, self.post_eng = NT, EC, phase_a, NB, NBUF, BGR, mask_eng, post_eng
        self.BPOOL = BPOOL
        self.npool = 6
        self.tiles = list(range(NTILE)) if tiles is None else tiles
        self.phase_b = phase_b
        self.debug = debug
        self.nc = bass.Bass("TRN2", target_bir_lowering=False)
        self.P = Prog(self.nc)
        self.dbg_outs = []

    def mm(self, ob, oap, lb, lap, rb, rap, start=True, stop=True):
        self.P.op("pe", lambda e: e.matmul(oap, lhsT=lap, rhs=rap, start=start, stop=stop),
                  reads=[lb, rb], writes=[ob], pe_accum=True)

    def tt(self, eng, ob, oap, ab, aap, bb, bap, op):
        self.P.op(eng, lambda e: e.tensor_tensor(out=oap, in0=aap, in1=bap, op=op), reads=[ab, bb], writes=[ob])

    def ts(self, eng, ob, oap, ab, aap, s1, s2, op0, op1=None, extra_reads=()):
        if op1 is None:
            self.P.op(eng, lambda e: e.tensor_scalar(out=oap, in0=aap, scalar1=s1, scalar2=None, op0=op0),
                      reads=[ab] + list(extra_reads), writes=[ob])
        else:
            self.P.op(eng, lambda e: e.tensor_scalar(out=oap, in0=aap, scalar1=s1, scalar2=s2, op0=op0, op1=op1),
                      reads=[ab] + list(extra_reads), writes=[ob])

    def act(self, ob, oap, ab, aap, func, scale=1.0, bias=0.0, extra_reads=()):
        self.P.op("act", lambda e: e.activation(out=oap, in_=aap, func=func, scale=scale, bias=bias),
                  reads=[ab] + list(extra_reads), writes=[ob])

    def cp(self, eng, ob, oap, ab, aap):
        if eng == "act":
            self.P.op("act", lambda e: e.copy(out=oap, in_=aap), reads=[ab], writes=[ob])
        else:
            self.P.op(eng, lambda e: e.tensor_copy(out=oap, in_=aap), reads=[ab], writes=[ob])

    def load(self, q, db, dap, sb, sap, nd=64):
        self.P.dma(q, lambda e: e.dma_start(out=dap, in_=sap), owner=db, reads=[sb], writes=[db], nd=nd)

    def store(self, q, db, dap, sb, sap, final=True):
        self.P.dma(q, lambda e: e.dma_start(out=dap, in_=sap), owner=sb, reads=[sb], writes=[db], final=final)

    def nb(self):
        b = self.banks[self.bank_i % self.npool]
        self.bank_i += 1
        return b

    def dbg(self, name, buf, ap, shape, dt=F32):
        if not self.debug:
            return
        o = self.P.dram(name, shape, dt, kind="ExternalOutput")
        self.dbg_outs.append(name)
        self.store("sp", o, o[:], buf, ap)

    def build(self):
        P = self.P
        with ExitStack() as st:
            P.stack = st
            self.declare_io()
            self.alloc_common()
            if self.phase_b:
                self.convert_tables()
            if self.do_a:
                with ExitStack() as sa:
                    P.stack = sa
                    self.phase_a()
                    P.barrier()
            if self.phase_b:
                with ExitStack() as sbk:
                    P.stack = sbk
                    self.phase_b_run()
                    P.barrier()
            P.stack = st
            P.emit(st)
        return self.nc

    def declare_io(self):
        P = self.P
        I = lambda n, s: P.dram(n, s, F32, kind="ExternalInput")
        O = lambda n, s: P.dram(n, s, F32, kind="ExternalOutput")
        self.xT = I("xT", [D, TT])
        self.x = I("x", [TT, D])
        self.pT = I("pT", [256, TT])
        self.w_in = I("w_in", [D, WIN])
        self.ropeC = I("ropeC", [128, TT])
        self.ropeS = I("ropeS", [128, TT])
        self.ropeCt = I("ropeCt", [256, 32])
        self.ropeSt = I("ropeSt", [256, 32])
        self.sinks = I("sinks", [128, 8])
        self.w_ap = I("w_ap", [512, D])
        self.w_sp = I("w_sp", [512, D])
        self.w_glu = I("w_glu", [512, 512])
        self.b_glu = I("b_glu", [128, 4])
        self.w_out = I("w_out", [D, D])
        self.ln1g = I("ln1g", [128, D])
        self.ln1b = I("ln1b", [128, D])
        self.lam_re = I("lam_re", [128, 16])
        self.lam_im = I("lam_im", [128, 16])
        self.logdt = I("logdt", [128, 16])
        self.Bre = I("Bre", [128, 16, 16])
        self.Bim = I("Bim", [128, 16, 16])
        self.Cre = I("Cre", [128, 16, 16])
        self.Cim = I("Cim", [128, 16, 16])
        self.dssm = I("dssm", [128, 4])
        self.wkT = I("wkT", [128, NSEQ, 128])
        self.wv = I("wv", [128, NSEQ, 128])
        self.wk_raw = I("wk_raw", [NSEQ, 128, 128])
        self.wv_raw = I("wv_raw", [NSEQ, 128, 128])
        self.h0re = I("h0re", [128, 16, NSEQ])
        self.h0im = I("h0im", [128, 16, NSEQ])
        self.mask_own = I("mask_own", [128, 512])
        self.mask_prev = I("mask_prev", [128, 512])
        self.mask_new = I("mask_new", [128, 512])
        self.mask_past = I("mask_past", [128, 32])
        self.iota_tau = I("iota_tau", [128, 128])
        self.w_q = I("w_q", [D, 2048])
        self.keysT = I("keysT", [128, 16, 128])
        self.uT = I("uT", [D, 16384])
        self.vtab = I("vtab", [16384, D])
        self.ln2g = I("ln2g", [128, D])
        self.ln2b = I("ln2b", [128, D])
        self.w_gate = I("w_gate", [D, D])
        self.w_pp = I("w_pp", [256, D])
        self.iota128 = I("iota128", [128, 128])
        self.iota16 = I("iota16", [128, 16])
        self.y = O("y", [TT, D])
        self.nk_p = O("nk_p", [128, 128])
        self.nv_p = O("nv_p", [128, 128])
        self.hre_p = O("hre_p", [128, 16])
        self.him_p = O("him_p", [128, 16])
        self.nk_s = O("nk_s", [NSEQ, 128, 128])
        self.nv_s = O("nv_s", [NSEQ, 128, 128])
        self.hre_s = O("hre_s", [128, 16, NSEQ])
        self.him_s = O("him_s", [128, 16, NSEQ])
        self.X1 = [P.dram(f"X1_{t}", [128, D], F32, kind=("ExternalOutput" if (self.debug and t == 0) else "Internal"))
                   for t in range(NTILE)]

    def alloc_common(self):
        P = self.P
        self.banks = [P.psum(f"bank{i}", [128, 512], F32) for i in range(8)]
        self.bank_i = 0
        self.identb = P.sbuf("identb", [128, 128], BF16)
        self.identf = P.sbuf("identf", [128, 128], F32)
        self.onesb = P.sbuf("onesb", [128, 64], BF16)
        for t_, in [(self.identb,), (self.identf,)]:
            P.op("pool", lambda e, t_=t_: e.memset(t_[:], 1.0), writes=[t_])
            P.op("pool", lambda e, t_=t_: e.affine_select(out=t_[:], in_=t_[:], pattern=[[-1, 128]],
                                                          compare_op=ALU.is_equal, fill=0.0, base=0,
                                                          channel_multiplier=1), reads=[t_], writes=[t_])
        P.op("pool", lambda e: e.memset(self.onesb[:], 1.0), writes=[self.onesb])

    def load_cast(self, db, dap_fn, sb, sap_fn, ncols, maxc=2048):
        c0 = 0
        while c0 < ncols:
            c1 = min(ncols, c0 + maxc)
            self.load("pool", db, dap_fn(c0, c1), sb, sap_fn(c0, c1))
            c0 = c1

    def phase_a(self):
        P = self.P
        sb = P.sbuf
        self.winb = sb("winb", [128, 8, WIN], BF16)
        self.load_cast(self.winb, lambda a, b: self.winb[:, :, a:b], self.w_in,
                       lambda a, b: self.w_in[:, a:b].rearrange("(k p) n -> p k n", p=128), WIN, 1984)
        self.wapb = sb("wapb", [64, 8, D], BF16)
        self.load("pool", self.wapb, self.wapb[:], self.w_ap, self.w_ap[:].rearrange("(h p) n -> p h n", p=64))
        self.wspb = sb("wspb", [128, 4, D], BF16)
        self.load("pool", self.wspb, self.wspb[:], self.w_sp, self.w_sp[:].rearrange("(k p) n -> p k n", p=128))
        self.wglub = sb("wglub", [128, 4, 512], BF16)
        self.load("pool", self.wglub, self.wglub[:], self.w_glu, self.w_glu[:].rearrange("(k p) n -> p k n", p=128))
        self.woutb = sb("woutb", [128, 8, D], BF16)
        self.load("pool", self.woutb, self.woutb[:], self.w_out, self.w_out[:].rearrange("(k p) n -> p k n", p=128))
        self.bglu = sb("bglu", [128, 4], F32)
        self.load("sp", self.bglu, self.bglu[:], self.b_glu, self.b_glu[:])
        self.dss = sb("dss", [128, 4], F32)
        self.load("sp", self.dss, self.dss[:], self.dssm, self.dssm[:])
        self.g1 = sb("g1", [128, D], F32)
        self.b1 = sb("b1", [128, D], F32)
        self.load("sp", self.g1, self.g1[:], self.ln1g, self.ln1g[:])
        self.load("sp", self.b1, self.b1[:], self.ln1b, self.ln1b[:])
        self.mown = sb("mown", [128, 512], BF16)
        self.mprev = sb("mprev", [128, 512], BF16)
        self.mnew = sb("mnew", [128, 512], BF16)
        self.mpast = sb("mpast", [128, 32], BF16)
        for d_, s_ in [(self.mown, self.mask_own), (self.mprev, self.mask_prev), (self.mnew, self.mask_new),
                       (self.mpast, self.mask_past)]:
            self.load("pool", d_, d_[:], s_, s_[:])
        self.esink = sb("esink", [128, 8], F32)
        self.load("sp", self.esink, self.esink[:], self.sinks, self.sinks[:])
        self.act(self.esink, self.esink[:], self.esink, self.esink[:], AF.Exp)
        self.ssm_prologue()
        self.xTb = sb("xTb", [128, 8, 128], BF16)
        self.xtok = sb("xtok", [128, D], F32)
        self.rc = sb("rc", [128, 128], F32)
        self.rs = sb("rs", [128, 128], F32)
        self.qTb = sb("qTb", [128, 4, 128], BF16)
        self.kTb = [sb(f"kTb{i}", [128, 128], BF16) for i in range(2)]
        self.vb = [sb(f"vb{i}", [128, 128], BF16) for i in range(2)]
        self.vf = sb("vf", [128, 128], F32)
        self.ktok = sb("ktok", [128, 2, 2, 32], F32)
        self.rct = sb("rct", [128, 32], F32)
        self.rst = sb("rst", [128, 32], F32)
        self.uTbs = [sb(f"uTb{i}", [128, 4, 128], BF16) for i in range(2)]
        self.uTb = self.uTbs[0]
        self.sga = sb("sga", [128, 8, 128], BF16)
        self.sgb = sb("sgb", [128, 8, 128], BF16)
        self.tA = sb("tA", [128, 512], F32)
        self.tB = sb("tB", [128, 512], F32)
        self.tC = sb("tC", [128, 512], F32)
        self.tD = sb("tD", [128, 512], F32)
        self.pA, self.pB = self.tA, self.tB
        self.pTo2 = [sb(f"pTo{i}", [128, 512], BF16) for i in range(2)]
        self.pTp2 = [sb(f"pTp{i}", [128, 512], BF16) for i in range(2)]
        self.pTo, self.pTp = self.pTo2[0], self.pTp2[0]
        self.aTb = sb("aTb", [64, 8, 128], BF16)
        self.rden = sb("rden", [64, 512], F32)
        self.numS = sb("numS", [64, 512], F32)
        self.xhr = sb("xhr", [128, 512], F32)
        self.xhi = sb("xhi", [128, 512], F32)
        self.ghr = sb("ghr", [128, 512], F32)
        self.ghi = sb("ghi", [128, 512], F32)
        self.hreb = sb("hreb", [128, 4, 128], BF16)
        self.himb = sb("himb", [128, 4, 128], BF16)
        self.carr = sb("carr", [128, 16], F32)
        self.cari = sb("cari", [128, 16], F32)
        self.gini_r = sb("gini_r", [128, 16], F32)
        self.gini_i = sb("gini_i", [128, 16], F32)
        self.c1 = sb("c1", [128, 16], F32)
        self.c2 = sb("c2", [128, 16], F32)
        self.c3 = sb("c3", [128, 16], F32)
        self.c4 = sb("c4", [128, 16], F32)
        self.ygb = sb("ygb", [128, 4, 128], BF16)
        self.sig = sb("sig", [128, 4, 128], BF16)
        self.sTb = sb("sTb", [128, 4, 128], BF16)
        self.mixb = sb("mixb", [128, 8, 128], BF16)
        self.mA = sb("mA", [128, 512], BF16)
        self.mB = sb("mB", [128, 512], BF16)
        self.x1 = sb("x1", [128, D], F32)
        self.r1 = self.x1
        self.stats = sb("stats", [128, 2, 6], F32)
        self.mv = sb("mv", [128, 2], F32)
        self.rstd = sb("rstd", [128, 1], F32)
        outer = P.stack
        with ExitStack() as pst:
            P.stack = pst
            self.Tc, self.Ts = sb("Tc", [128, 16, 128], F32), sb("Ts", [128, 16, 128], F32)
            print("phase A (prompt) sbuf bytes remaining:", self.nc.sbuf_bytes_remaining)
            self.ssm_prologue_run()
            P.op("dve", lambda e: e.memset(self.carr[:], 0.0), writes=[self.carr])
            P.op("dve", lambda e: e.memset(self.cari[:], 0.0), writes=[self.cari])
            self.run_tiles_a([t for t in self.tiles if t < NTILE - 1])
            P.barrier()
        P.stack = outer
        self.kbT = sb("kbT", [128, NSEQ, 128], BF16)
        self.vbuf = sb("vbuf", [128, NSEQ, 128], BF16)
        self.h0r = sb("h0r", [128, 16, NSEQ], F32)
        self.h0i = sb("h0i", [128, 16, NSEQ], F32)
        self.hfr = sb("hfr", [128, 16, NSEQ], F32)
        self.hfi = sb("hfi", [128, 16, NSEQ], F32)
        self.s1 = sb("s1", [128, 4, NSEQ], F32)
        self.s2 = sb("s2", [128, 4, NSEQ], F32)
        print("phase A sbuf bytes remaining:", self.nc.sbuf_bytes_remaining)
        if NTILE - 1 in self.tiles:
            self.run_tiles_a([NTILE - 1])

    def range_reduce2(self, angB, ang, tfB, tf, tiB, ti):
        P = self.P
        self.ts("dve", tfB, tf, angB, ang, 1.0 / TWO_PI, None, ALU.mult)
        self.cp("dve", tiB, ti, tfB, tf)
        self.cp("dve", tfB, tf, tiB, ti)
        P.op("dve", lambda e: e.scalar_tensor_tensor(out=ang, in0=tf, scalar=-TWO_PI, in1=ang,
                                                     op0=ALU.mult, op1=ALU.add), reads=[tfB, angB], writes=[angB])
        self.ts("dve", tfB, tf, angB, ang, PI, -TWO_PI, ALU.is_gt, ALU.mult)
        self.tt("dve", angB, ang, angB, ang, tfB, tf, ALU.add)
        self.ts("dve", tfB, tf, angB, ang, -PI, TWO_PI, ALU.is_lt, ALU.mult)
        self.tt("dve", angB, ang, angB, ang, tfB, tf, ALU.add)
        self.ts("dve", angB, ang, angB, ang, 3.14159, -3.14159, ALU.min, ALU.max)

    def range_reduce(self, ang, shape, tmpf, tmpi):
        P = self.P
        self.ts("dve", tmpf, tmpf[:], ang, ang[:], 1.0 / TWO_PI, None, ALU.mult)
        self.cp("dve", tmpi, tmpi[:], tmpf, tmpf[:])
        self.cp("dve", tmpf, tmpf[:], tmpi, tmpi[:])
        P.op("dve", lambda e: e.scalar_tensor_tensor(out=ang[:], in0=tmpf[:], scalar=-TWO_PI, in1=ang[:],
                                                     op0=ALU.mult, op1=ALU.add), reads=[tmpf, ang], writes=[ang])
        self.ts("dve", tmpf, tmpf[:], ang, ang[:], PI, -TWO_PI, ALU.is_gt, ALU.mult)
        self.tt("dve", ang, ang[:], ang, ang[:], tmpf, tmpf[:], ALU.add)
        self.ts("dve", tmpf, tmpf[:], ang, ang[:], -PI, TWO_PI, ALU.is_lt, ALU.mult)
        self.tt("dve", ang, ang[:], ang, ang[:], tmpf, tmpf[:], ALU.add)
        self.ts("dve", ang, ang[:], ang, ang[:], 3.14159, -3.14159, ALU.min, ALU.max)

    def ssm_prologue(self):
        P = self.P
        S = [128, 16]
        S4 = [128, 16, 128]
        self.mag, self.sth, self.cth = P.sbuf("mag", S, F32), P.sbuf("sth", S, F32), P.sbuf("cth", S, F32)
        self.are, self.aim = P.sbuf("are", S, F32), P.sbuf("aim", S, F32)
        self.KBT = [P.sbuf("KBTre", S4, BF16), P.sbuf("KBTim", S4, BF16)]
        self.CT = [P.sbuf("CTre", S4, BF16), P.sbuf("CTimn", S4, BF16)]

    def ssm_prologue_run(self):
        P = self.P
        outer = P.stack
        with ExitStack() as tmp:
            P.stack = tmp
            self._ssm_prologue_body()
            P.barrier()
        P.stack = outer

    def _ssm_prologue_body(self):
        P = self.P
        sb = P.sbuf
        S = [128, 16]
        lr, li, ld = sb("lr", S, F32), sb("li", S, F32), sb("ld", S, F32)
        for d_, s_ in [(lr, self.lam_re), (li, self.lam_im), (ld, self.logdt)]:
            self.load("sp", d_, d_[:], s_, s_[:])
        dt = sb("dt", S, F32)
        self.act(dt, dt[:], ld, ld[:], AF.Exp)
        t0, t1, t2 = sb("p_t0", S, F32), sb("p_t1", S, F32), sb("p_t2", S, F32)
        ti = sb("p_ti", S, I32)
        self.tt("dve", t0, t0[:], lr, lr[:], dt, dt[:], ALU.mult)
        self.act(self.mag, self.mag[:], t0, t0[:], AF.Exp)
        th = sb("th", S, F32)
        self.tt("dve", th, th[:], li, li[:], dt, dt[:], ALU.mult)
        ang = sb("p_ang", S, F32)
        self.cp("dve", ang, ang[:], th, th[:])
        self.range_reduce(ang, S, t1, ti)
        self.act(self.sth, self.sth[:], ang, ang[:], AF.Sin)
        self.ts("dve", ang, ang[:], th, th[:], PI / 2.0, None, ALU.add)
        self.range_reduce(ang, S, t1, ti)
        self.act(self.cth, self.cth[:], ang, ang[:], AF.Sin)
        self.tt("dve", self.are, self.are[:], self.mag, self.mag[:], self.cth, self.cth[:], ALU.mult)
        self.tt("dve", self.aim, self.aim[:], self.mag, self.mag[:], self.sth, self.sth[:], ALU.mult)
        nr, den, kre, kim = sb("nr", S, F32), sb("den", S, F32), sb("kre", S, F32), sb("kim", S, F32)
        self.ts("dve", nr, nr[:], self.are, self.are[:], -1.0, None, ALU.add)
        self.tt("dve", t0, t0[:], lr, lr[:], lr, lr[:], ALU.mult)
        self.tt("dve", t1, t1[:], li, li[:], li, li[:], ALU.mult)
        self.tt("dve", den, den[:], t0, t0[:], t1, t1[:], ALU.add)
        P.op("dve", lambda e: e.reciprocal(out=den[:], in_=den[:]), reads=[den], writes=[den])
        self.tt("dve", t0, t0[:], nr, nr[:], lr, lr[:], ALU.mult)
        self.tt("dve", t1, t1[:], self.aim, self.aim[:], li, li[:], ALU.mult)
        self.tt("dve", t2, t2[:], t0, t0[:], t1, t1[:], ALU.add)
        self.tt("dve", kre, kre[:], t2, t2[:], den, den[:], ALU.mult)
        self.tt("dve", t0, t0[:], self.aim, self.aim[:], lr, lr[:], ALU.mult)
        self.tt("dve", t1, t1[:], nr, nr[:], li, li[:], ALU.mult)
        self.tt("dve", t2, t2[:], t0, t0[:], t1, t1[:], ALU.subtract)
        self.tt("dve", kim, kim[:], t2, t2[:], den, den[:], ALU.mult)
        S3 = [128, 16, 16]
        r3 = lambda ap: ap.rearrange("p (s c) -> p s c", s=16)
        brB, br = self.xhr, r3(self.xhr[:, 0:256])
        biB, bi = self.xhr, r3(self.xhr[:, 256:512])
        kbrB, kbr = self.xhi, r3(self.xhi[:, 0:256])
        kbiB, kbi = self.xhi, r3(self.xhi[:, 256:512])
        u0B, u0 = self.ghr, r3(self.ghr[:, 0:256])
        u1B, u1 = self.ghr, r3(self.ghr[:, 256:512])
        crB, cr = self.ghi, r3(self.ghi[:, 0:256])
        ciB, ci = self.ghi, r3(self.ghi[:, 256:512])
        self.load("sp", brB, br, self.Bre, self.Bre[:])
        self.load("sp", biB, bi, self.Bim, self.Bim[:])
        krb = kre[:].unsqueeze(2).to_broadcast(S3)
        kib = kim[:].unsqueeze(2).to_broadcast(S3)
        self.tt("dve", u0B, u0, brB, br, kre, krb, ALU.mult)
        self.tt("dve", u1B, u1, biB, bi, kim, kib, ALU.mult)
        self.tt("dve", kbrB, kbr, u0B, u0, u1B, u1, ALU.subtract)
        self.tt("dve", u0B, u0, biB, bi, kre, krb, ALU.mult)
        self.tt("dve", u1B, u1, brB, br, kim, kib, ALU.mult)
        self.tt("dve", kbiB, kbi, u0B, u0, u1B, u1, ALU.add)
        padB = self.xtok
        pad = self.xtok[:].bitcast(BF16).rearrange("p (s n) -> p s n", s=16)
        for ri, (srcB, src) in enumerate([(kbrB, kbr), (kbiB, kbi)]):
            P.op("dve", lambda e: e.memset(pad, 0.0), writes=[padB])
            for qd in range(4):
                for hf in range(2):
                    c0 = 32 * qd + 16 * hf
                    self.cp("dve", padB, pad[64 * hf:64 * hf + 64, qd::4, c0:c0 + 16], srcB,
                            src[64 * hf:64 * hf + 64, qd::4, :])
            for grp in range(2):
                bk = self.nb()
                bkb = bk[:].bitcast(BF16)
                for j in range(8):
                    stt = grp * 8 + j
                    P.op("pe", lambda e, stt=stt, j=j, bkb=bkb: e.transpose(out=bkb[:, j * 128:(j + 1) * 128],
                                                                            in_=pad[:, stt, :], identity=self.identb[:]),
                         reads=[padB, self.identb], writes=[bk], pe_accum=True)
                self.cp("act", self.KBT[ri], self.KBT[ri][:, grp * 8:(grp + 1) * 8, :], bk,
                        bkb.rearrange("p (j n) -> p j n", j=8))
        self.load("sp", crB, cr, self.Cre, self.Cre[:])
        self.load("sp", ciB, ci, self.Cim, self.Cim[:])
        self.ts("dve", ciB, ci, ciB, ci, -1.0, None, ALU.mult)
        for ri, (srcB, src) in enumerate([(crB, cr), (ciB, ci)]):
            P.op("dve", lambda e, ri=ri: e.memset(self.CT[ri][:], 0.0), writes=[self.CT[ri]])
            for qd in range(4):
                for hf in range(2):
                    c0 = 32 * qd + 16 * hf
                    self.cp("dve", self.CT[ri], self.CT[ri][64 * hf:64 * hf + 64, qd::4, c0:c0 + 16], srcB,
                            src[64 * hf:64 * hf + 64, qd::4, :])
        io = sb("iotau", [128, 128], F32)
        self.load("sp", io, io[:], self.iota_tau, self.iota_tau[:])
        S4 = [128, 4, 128]
        v3 = lambda b: b[:].rearrange("p (j n) -> p j n", j=4)
        iob = io[:].unsqueeze(1).to_broadcast(S4)
        tiB = self.tC
        for sg in range(4):
            ss = slice(4 * sg, 4 * sg + 4)
            thb = th[:, ss].unsqueeze(2).to_broadcast(S4)
            self.tt("dve", self.tD, v3(self.tD), th, thb, io, iob, ALU.mult)
            self.cp("dve", self.tA, self.tA[:], self.tD, self.tD[:])
            self.range_reduce2(self.tA, self.tA[:], self.tB, self.tB[:], tiB, tiB[:].bitcast(I32))
            self.act(self.Ts, self.Ts[:, ss, :], self.tA, v3(self.tA), AF.Sin)
            self.ts("dve", self.tA, self.tA[:], self.tD, self.tD[:], PI / 2.0, None, ALU.add)
            self.range_reduce2(self.tA, self.tA[:], self.tB, self.tB[:], tiB, tiB[:].bitcast(I32))
            self.act(self.Tc, self.Tc[:, ss, :], self.tA, v3(self.tA), AF.Sin)

    def tile_a_gen(self, t):
        P = self.P
        is_s = (t == NTILE - 1)
        c0 = 128 * t
        cur, prv = t % 2, (t + 1) % 2
        kT, kTprev, vb, vbprev = self.kTb[cur], self.kTb[prv], self.vb[cur], self.vb[prv]
        self.uTb = self.uTbs[cur]
        self.load("pool", self.xTb, self.xTb[:], self.xT, self.xT[:, c0:c0 + 128].rearrange("(k p) n -> p k n", p=128))
        if self.phase_b:
            self.convert_some(1)
        self.load("sp", self.rc, self.rc[:], self.ropeC, self.ropeC[:, c0:c0 + 128])
        self.load("sp", self.rs, self.rs[:], self.ropeS, self.ropeS[:, c0:c0 + 128])

        def proj_fm(bank, j, col):
            for kt in range(8):
                self.mm(bank, bank[:, j * 128:(j + 1) * 128], self.winb, self.winb[:, kt, col:col + 128],
                        self.xTb, self.xTb[:, kt, :], start=(kt == 0), stop=(kt == 7))

        bq, bqp = self.nb(), self.nb()
        for j in range(4):
            proj_fm(bq, j, QO + 128 * j)
        for j in range(4):
            proj_fm(bqp, j, QP + 128 * j)
        rcb = self.rc[:].unsqueeze(1).to_broadcast([128, 4, 128])
        rsb = self.rs[:].unsqueeze(1).to_broadcast([128, 4, 128])
        v3 = lambda b: b[:].rearrange("p (j n) -> p j n", j=4)
        self.tt("dve", self.tA, v3(self.tA), bq, v3(bq), self.rc, rcb, ALU.mult)
        self.tt("dve", self.tB, v3(self.tB), bqp, v3(bqp), self.rs, rsb, ALU.mult)
        self.tt("dve", self.qTb, self.qTb[:], self.tA, v3(self.tA), self.tB, v3(self.tB), ALU.add)
        yield "H"
        bk = self.nb()
        proj_fm(bk, 0, KO)
        proj_fm(bk, 1, KP)
        self.tt("dve", self.tC, self.tC[:, 0:128], bk, bk[:, 0:128], self.rc, self.rc[:], ALU.mult)
        self.tt("dve", self.tD, self.tD[:, 0:128], bk, bk[:, 128:256], self.rs, self.rs[:], ALU.mult)
        self.tt("dve", kT, kT[:], self.tC, self.tC[:, 0:128], self.tD, self.tD[:, 0:128], ALU.add)
        yield "H"
        bv = self.nb()
        for kt in range(8):
            self.mm(bv, bv[:, 0:128], self.xTb, self.xTb[:, kt, :], self.winb, self.winb[:, kt, VO:VO + 128],
                    start=(kt == 0), stop=(kt == 7))
        self.cp("act", vb, vb[:], bv, bv[:, 0:128])
        if t >= NTILE - 2:
            self.cp("act", self.vf, self.vf[:], bv, bv[:, 0:128])
            bkt = self.nb()
            for kt in range(8):
                self.mm(bkt, bkt[:, 0:128], self.xTb, self.xTb[:, kt, :], self.winb, self.winb[:, kt, KO:KO + 128],
                        start=(kt == 0), stop=(kt == 7))
            r0 = 128 * (t - (NTILE - 2))
            self.load("sp", self.rct, self.rct[:], self.ropeCt, self.ropeCt[r0:r0 + 128, :])
            self.load("sp", self.rst, self.rst[:], self.ropeSt, self.ropeSt[r0:r0 + 128, :])
            k4 = bkt[:, 0:128].rearrange("p (g h j) -> p g h j", g=2, h=2)
            cb = self.rct[:].unsqueeze(1).to_broadcast([128, 2, 32])
            sbb = self.rst[:].unsqueeze(1).to_broadcast([128, 2, 32])
            a4 = lambda b: b[:, 0:64].rearrange("p (g j) -> p g j", g=2)
            self.tt("dve", self.tA, a4(self.tA), bkt, k4[:, :, 0, :], self.rct, cb, ALU.mult)
            self.tt("dve", self.tB, a4(self.tB), bkt, k4[:, :, 1, :], self.rst, sbb, ALU.mult)
            self.tt("dve", self.ktok, self.ktok[:, :, 0, :], self.tA, a4(self.tA), self.tB, a4(self.tB), ALU.subtract)
            self.tt("dve", self.tA, a4(self.tA), bkt, k4[:, :, 0, :], self.rst, sbb, ALU.mult)
            self.tt("dve", self.tB, a4(self.tB), bkt, k4[:, :, 1, :], self.rct, cb, ALU.mult)
            self.tt("dve", self.ktok, self.ktok[:, :, 1, :], self.tA, a4(self.tA), self.tB, a4(self.tB), ALU.add)
            kflat = self.ktok[:].rearrange("p g h j -> p (g h j)")
            if not is_s:
                self.store("sp", self.nk_p, self.nk_p[:], self.ktok, kflat)
                self.store("sp", self.nv_p, self.nv_p[:], self.vf, self.vf[:])
            else:
                for s in range(NSEQ):
                    self.store("sp", self.nk_s, self.nk_s[s, 120:128, :], self.ktok, kflat[8 * s:8 * s + 8, :])
                    self.store("sp", self.nv_s, self.nv_s[s, 120:128, :], self.vf, self.vf[8 * s:8 * s + 8, :])
                    P.dma("sp", lambda e, s=s: e.dma_start(out=self.nk_s[s, 0:120, :], in_=self.wk_raw[s, 8:128, :]),
                          owner=self.nk_s, reads=[self.wk_raw], writes=[self.nk_s], final=True)
                    P.dma("sp", lambda e, s=s: e.dma_start(out=self.nv_s[s, 0:120, :], in_=self.wv_raw[s, 8:128, :]),
                          owner=self.nv_s, reads=[self.wv_raw], writes=[self.nv_s], final=True)
        yield "H"
        bu = self.nb()
        for j in range(4):
            proj_fm(bu, j, UO + 128 * j)
        self.cp("act", self.uTb, self.uTb[:], bu, v3(bu))
        yield "END_HEAD"
        self.uTb = self.uTbs[cur]
        self.ssm_tile(t, is_s)
        if is_s:
            self.attn_sample(kT, vb)
        else:
            self.attn_prompt(t, kT, kTprev, vb, vbprev)
        for gi, (col, dst) in enumerate([(GA, self.sga), (GB, self.sgb)]):
            for hh in range(2):
                bg = self.nb()
                for j in range(4):
                    proj_fm(bg, j, col + 512 * hh + 128 * j)
                self.act(dst, dst[:, 4 * hh:4 * hh + 4, :], bg, v3(bg), AF.Sigmoid)
        yield "T2"
        for hh in range(2):
            bA, bS = self.nb(), self.nb()
            for j in range(4):
                dcol = 512 * hh + 128 * j
                for h in range(8):
                    self.mm(bA, bA[:, j * 128:(j + 1) * 128], self.wapb, self.wapb[:, h, dcol:dcol + 128],
                            self.aTb, self.aTb[:, h, :], start=(h == 0), stop=(h == 7))
                for ci in range(4):
                    self.mm(bS, bS[:, j * 128:(j + 1) * 128], self.wspb, self.wspb[:, ci, dcol:dcol + 128],
                            self.sTb, self.sTb[:, ci, :], start=(ci == 0), stop=(ci == 3))
            self.tt("dve", self.mA, v3(self.mA), bA, v3(bA), self.sga, self.sga[:, 4 * hh:4 * hh + 4, :], ALU.mult)
            self.tt("dve", self.mB, v3(self.mB), bS, v3(bS), self.sgb, self.sgb[:, 4 * hh:4 * hh + 4, :], ALU.mult)
            self.tt("dve", self.mixb, self.mixb[:, 4 * hh:4 * hh + 4, :], self.mA, v3(self.mA), self.mB, v3(self.mB), ALU.add)
            yield "T"
        self.load("sp", self.xtok, self.xtok[:], self.x, self.x[c0:c0 + 128, :])
        for hh in range(2):
            by = self.nb()
            for kt in range(8):
                self.mm(by, by[:], self.mixb, self.mixb[:, kt, :], self.woutb, self.woutb[:, kt, 512 * hh:512 * hh + 512],
                        start=(kt == 0), stop=(kt == 7))
            P.op("dve", lambda e, hh=hh, by=by: e.scalar_tensor_tensor(
                out=self.r1[:, 512 * hh:512 * hh + 512], in0=self.xtok[:, 512 * hh:512 * hh + 512], scalar=ALPHA,
                in1=by[:], op0=ALU.mult, op1=ALU.add), reads=[self.xtok, by], writes=[self.r1])
            yield "T"
        self.layer_norm(self.r1, self.x1, self.g1, self.b1)
        self.store("sp", self.X1[t], self.X1[t][:], self.x1, self.x1[:], final=False)
        if self.debug and t in (0, 1, NTILE - 1):
            self.dbg(f"dbg_x1_{t}", self.x1, self.x1[:], [128, D])

    def run_tiles_a(self, tl):
        if not tl:
            return
        gens = {t: self.tile_a_gen(t) for t in tl}

        def run_until(g, marker):
            for m in g:
                if m == marker:
                    return

        run_until(gens[tl[0]], "END_HEAD")
        for i, t in enumerate(tl):
            run_until(gens[t], "T2")
            g2 = gens[t]
            gh = gens[tl[i + 1]] if i + 1 < len(tl) else None
            d2, dh = False, gh is None
            while not (d2 and dh):
                if not d2:
                    try:
                        next(g2)
                    except StopIteration:
                        d2 = True
                if not dh:
                    if next(gh) == "END_HEAD":
                        dh = True

    def layer_norm(self, src, dst, g, b, part=0):
        P = self.P
        stats, mv, rstd = self.stats, self.mv, self.rstd
        for hh in range(2):
            P.op("dve", lambda e, hh=hh: e.bn_stats(out=stats[:, hh, :], in_=src[:, 512 * hh:512 * hh + 512]),
                 reads=[src], writes=[stats])
        P.op("dve", lambda e: e.bn_aggr(out=mv[:], in_=stats[:]), reads=[stats], writes=[mv])
        self.act(rstd, rstd[:], mv, mv[:, 1:2], AF.Sqrt, bias=LN_EPS)
        if part == 1:
            return
        self.layer_norm_fin(src, dst, g, b)

    def layer_norm_fin(self, src, dst, g, b):
        P = self.P
        stats, mv, rstd = self.stats, self.mv, self.rstd
        P.op("dve", lambda e: e.reciprocal(out=rstd[:], in_=rstd[:]), reads=[rstd], writes=[rstd])
        self.ts("dve", dst, dst[:], src, src[:], mv[:, 0:1], rstd[:, 0:1], ALU.subtract, ALU.mult,
                extra_reads=[mv, rstd])
        self.tt("dve", dst, dst[:], dst, dst[:], g, g[:], ALU.mult)
        self.tt("dve", dst, dst[:], dst, dst[:], b, b[:], ALU.add)

    def attn_finish(self, g, bn, bd, extra=None):
        P = self.P
        r3 = lambda ap: ap.rearrange("p (h n) -> p h n", h=4)
        r4 = lambda ap: ap.rearrange("p (h s q) -> p h s q", h=4, s=NSEQ)
        esb = self.esink[0:64, 4 * g:4 * g + 4].unsqueeze(2).to_broadcast([64, 4, 128])
        self.tt("dve", self.rden, r3(self.rden[:]), bd, r3(bd[0:64, :]), self.esink, esb, ALU.add)
        if extra is not None:
            self.tt("dve", self.rden, r4(self.rden[:]), self.rden, r4(self.rden[:]), extra[2], extra[3], ALU.add)
        P.op("dve", lambda e: e.reciprocal(out=self.rden[:], in_=self.rden[:]), reads=[self.rden], writes=[self.rden])
        if extra is None:
            self.tt("dve", self.aTb, self.aTb[:, 4 * g:4 * g + 4, :], bn, r3(bn[0:64, :]), self.rden, r3(self.rden[:]), ALU.mult)
        else:
            self.tt("dve", self.numS, r4(self.numS[:]), bn, r4(bn[0:64, :]), extra[0], extra[1], ALU.add)
            self.tt("dve", self.aTb, self.aTb[:, 4 * g:4 * g + 4, :], self.numS, r3(self.numS[:]), self.rden, r3(self.rden[:]), ALU.mult)

    def attn_prompt(self, t, kT, kTprev, vb, vbprev):
        for g in range(2):
            pr = slice(64 * g, 64 * g + 64)
            bs = self.nb()
            q3 = self.qTb[pr, :, :]
            o3 = bs[:].rearrange("p (h n) -> p h n", h=4)
            self.mm(bs, o3, kT, kT[pr, :], self.qTb, q3, start=True, stop=False)
            self.mm(bs, bs[:], self.identb, self.identb[:], self.mown, self.mown[:], start=False, stop=True)
            self.act(self.pTo2[g], self.pTo2[g][:], bs, bs[:], AF.Exp, scale=0.125)
            if t > 0:
                bs2 = self.nb()
                o32 = bs2[:].rearrange("p (h n) -> p h n", h=4)
                self.mm(bs2, o32, kTprev, kTprev[pr, :], self.qTb, q3, start=True, stop=False)
                self.mm(bs2, bs2[:], self.identb, self.identb[:], self.mprev, self.mprev[:], start=False, stop=True)
                self.act(self.pTp2[g], self.pTp2[g][:], bs2, bs2[:], AF.Exp, scale=0.125)
        for g in range(2):
            pr = slice(64 * g, 64 * g + 64)
            pTo, pTp = self.pTo2[g], self.pTp2[g]
            bn, bd = self.nb(), self.nb()
            self.mm(bn, bn[0:64, :], vb, vb[:, pr], pTo, pTo[:], start=True, stop=(t == 0))
            if t > 0:
                self.mm(bn, bn[0:64, :], vbprev, vbprev[:, pr], pTp, pTp[:], start=False, stop=True)
            self.mm(bd, bd[0:64, :], self.onesb, self.onesb[:], pTo, pTo[:], start=True, stop=(t == 0))
            if t > 0:
                self.mm(bd, bd[0:64, :], self.onesb, self.onesb[:], pTp, pTp[:], start=False, stop=True)
            self.attn_finish(g, bn, bd)

    def attn_sample(self, kT, vb):
        P = self.P
        self.load("pool", self.kbT, self.kbT[:], self.wkT, self.wkT[:], nd=8)
        self.load("pool", self.vbuf, self.vbuf[:], self.wv, self.wv[:], nd=8)
        for g in range(2):
            pr = slice(64 * g, 64 * g + 64)
            q3 = self.qTb[pr, :, :]
            pTo, pTp = self.pTo2[g], self.pTp2[g]
            bs = self.nb()
            self.mm(bs, bs[:].rearrange("p (h n) -> p h n", h=4), kT, kT[pr, :], self.qTb, q3, start=True, stop=False)
            self.mm(bs, bs[:], self.identb, self.identb[:], self.mnew, self.mnew[:], start=False, stop=True)
            self.act(pTo, pTo[:], bs, bs[:], AF.Exp, scale=0.125)
            bp = self.nb()
            for s in range(NSEQ):
                cs = slice(32 * s, 32 * s + 32)
                self.mm(bp, bp[:, cs], self.kbT, self.kbT[pr, s, :], self.qTb, self.qTb[pr, :, 8 * s:8 * s + 8],
                        start=True, stop=False)
                self.mm(bp, bp[:, cs], self.identb, self.identb[:], self.mpast, self.mpast[:], start=False, stop=True)
            self.act(pTp, pTp[:], bp, bp[:], AF.Exp, scale=0.125)
            bn, bd, bn2, bd2 = self.nb(), self.nb(), self.nb(), self.nb()
            self.mm(bn, bn[0:64, :], vb, vb[:, pr], pTo, pTo[:])
            self.mm(bd, bd[0:64, :], self.onesb, self.onesb[:], pTo, pTo[:])
            for s in range(NSEQ):
                cs = slice(32 * s, 32 * s + 32)
                self.mm(bn2, bn2[0:64, cs], self.vbuf, self.vbuf[:, s, pr], pTp, pTp[:, cs])
            self.mm(bd2, bd2[0:64, :], self.onesb, self.onesb[:], pTp, pTp[:])
            self.cp("act", self.tC, self.tC[0:64, :], bn2, bn2[0:64, :])
            self.cp("act", self.tD, self.tD[0:64, :], bd2, bd2[0:64, :])
            perm = lambda ap: ap.rearrange("p (s h q) -> p h s q", s=NSEQ, h=4)
            self.attn_finish(g, bn, bd, extra=(self.tC, perm(self.tC[0:64, :]), self.tD, perm(self.tD[0:64, :])))

    def ssm_tile(self, t, is_s):
        P = self.P
        v3 = lambda b: b[:].rearrange("p (j n) -> p j n", j=4)
        if is_s:
            self.load("sp", self.h0r, self.h0r[:], self.h0re, self.h0re[:])
            self.load("sp", self.h0i, self.h0i[:], self.h0im, self.h0im[:])
        by = self.banks[6]

        def xmm(sg_):
            br_, bi_ = self.nb(), self.nb()
            for j in range(4):
                stt = 4 * sg_ + j
                self.mm(br_, br_[:, j * 128:(j + 1) * 128], self.KBT[0], self.KBT[0][:, stt, :], self.uTb, self.uTb[:, sg_, :])
                self.mm(bi_, bi_[:, j * 128:(j + 1) * 128], self.KBT[1], self.KBT[1][:, stt, :], self.uTb, self.uTb[:, sg_, :])
            return br_, bi_

        nxt = xmm(0)
        for sg in range(4):
            bxr, bxi = nxt
            if sg + 1 < 4:
                nxt = xmm(sg + 1)
            ss = slice(4 * sg, 4 * sg + 4)
            if is_s:
                self.cp("act", self.xhr, self.xhr[:], bxr, bxr[:])
                self.cp("act", self.xhi, self.xhi[:], bxi, bxi[:])
                S = [128, 4, NSEQ]
                arb = self.are[:, ss].unsqueeze(2).to_broadcast(S)
                aib = self.aim[:, ss].unsqueeze(2).to_broadcast(S)
                v4 = lambda b: b[:].rearrange("p (s q t) -> p s q t", s=4, t=DSEQ)
                q1, q2 = self.s1[:, 0:4, :], self.s2[:, 0:4, :]
                for tt_ in range(DSEQ):
                    pr_, pi_ = (self.h0r, self.h0i) if tt_ == 0 else (self.ghr, self.ghi)
                    pra = self.h0r[:, ss, :] if tt_ == 0 else v4(self.ghr)[:, :, :, tt_ - 1]
                    pia = self.h0i[:, ss, :] if tt_ == 0 else v4(self.ghi)[:, :, :, tt_ - 1]
                    self.tt("dve", self.s1, q1, pr_, pra, self.are, arb, ALU.mult)
                    self.tt("dve", self.s2, q2, pi_, pia, self.aim, aib, ALU.mult)
                    self.tt("dve", self.s1, q1, self.s1, q1, self.s2, q2, ALU.subtract)
                    self.tt("dve", self.s2, q2, pi_, pia, self.are, arb, ALU.mult)
                    self.tt("dve", self.tA, v4(self.tA)[:, :, :, tt_], self.s1, q1, self.xhr, v4(self.xhr)[:, :, :, tt_], ALU.add)
                    self.tt("dve", self.s1, q1, pr_, pra, self.aim, aib, ALU.mult)
                    self.tt("dve", self.s1, q1, self.s1, q1, self.s2, q2, ALU.add)
                    self.tt("dve", self.ghi, v4(self.ghi)[:, :, :, tt_], self.s1, q1, self.xhi, v4(self.xhi)[:, :, :, tt_], ALU.add)
                    self.cp("dve", self.ghr, v4(self.ghr)[:, :, :, tt_], self.tA, v4(self.tA)[:, :, :, tt_])
                self.cp("act", self.hreb, self.hreb[:], self.ghr, v3(self.ghr))
                self.cp("act", self.himb, self.himb[:], self.ghi, v3(self.ghi))
                self.cp("dve", self.hfr, self.hfr[:, ss, :], self.ghr, v4(self.ghr)[:, :, :, DSEQ - 1])
                self.cp("dve", self.hfi, self.hfi[:, ss, :], self.ghi, v4(self.ghi)[:, :, :, DSEQ - 1])
            else:
                tc, tsn = self.Tc[:, ss, :], self.Ts[:, ss, :]
                self.tt("dve", self.tA, v3(self.tA), bxr, v3(bxr), self.Tc, tc, ALU.mult)
                self.tt("dve", self.tB, v3(self.tB), bxi, v3(bxi), self.Ts, tsn, ALU.mult)
                self.tt("dve", self.ghr, v3(self.ghr), bxi, v3(bxi), self.Tc, tc, ALU.mult)
                self.tt("dve", self.ghi, v3(self.ghi), bxr, v3(bxr), self.Ts, tsn, ALU.mult)
                self.tt("dve", self.xhr, self.xhr[:], self.tA, self.tA[:], self.tB, self.tB[:], ALU.add)
                self.tt("dve", self.xhi, self.xhi[:], self.ghr, self.ghr[:], self.ghi, self.ghi[:], ALU.subtract)
                cr, ci = self.carr[:, ss], self.cari[:, ss]
                ct_, sn = self.cth[:, ss], self.sth[:, ss]
                self.tt("dve", self.c1, self.c1[:, ss], self.carr, cr, self.cth, ct_, ALU.mult)
                self.tt("dve", self.c2, self.c2[:, ss], self.cari, ci, self.sth, sn, ALU.mult)
                self.tt("dve", self.c3, self.c3[:, ss], self.carr, cr, self.sth, sn, ALU.mult)
                self.tt("dve", self.c4, self.c4[:, ss], self.cari, ci, self.cth, ct_, ALU.mult)
                self.tt("dve", self.gini_r, self.gini_r[:, ss], self.c1, self.c1[:, ss], self.c2, self.c2[:, ss], ALU.subtract)
                self.tt("dve", self.gini_i, self.gini_i[:, ss], self.c3, self.c3[:, ss], self.c4, self.c4[:, ss], ALU.add)
                for j in range(4):
                    stt = 4 * sg + j
                    mb = self.mag[:, stt:stt + 1].to_broadcast([128, 128])
                    for (dst, src, ini) in [(self.ghr, self.xhr, self.gini_r), (self.ghi, self.xhi, self.gini_i)]:
                        P.op("dve", lambda e, dst=dst, src=src, ini=ini, j=j, stt=stt, mb=mb: e.tensor_tensor_scan(
                            out=dst[:, j * 128:(j + 1) * 128], data0=mb, data1=src[:, j * 128:(j + 1) * 128],
                            initial=ini[:, stt:stt + 1], op0=ALU.mult, op1=ALU.add),
                            reads=[self.mag, src, ini], writes=[dst])
                pe_ = self.post_eng
                self.tt(pe_, self.pA, v3(self.pA), self.ghr, v3(self.ghr), self.Tc, tc, ALU.mult)
                self.tt(pe_, self.pB, v3(self.pB), self.ghi, v3(self.ghi), self.Ts, tsn, ALU.mult)
                self.tt(pe_, self.tD, v3(self.tD), self.ghi, v3(self.ghi), self.Tc, tc, ALU.mult)
                self.tt(pe_, self.xhr, v3(self.xhr), self.ghr, v3(self.ghr), self.Ts, tsn, ALU.mult)
                self.tt(pe_, self.tC, self.tC[:], self.pA, self.pA[:], self.pB, self.pB[:], ALU.subtract)
                self.tt(pe_, self.tD, self.tD[:], self.tD, self.tD[:], self.xhr, self.xhr[:], ALU.add)
                self.cp("act", self.hreb, self.hreb[:], self.tC, v3(self.tC))
                self.cp("act", self.himb, self.himb[:], self.tD, v3(self.tD))
                self.cp("act", self.carr, self.carr[:, ss], self.tC, v3(self.tC)[:, :, 127])
                self.cp("act", self.cari, self.cari[:, ss], self.tD, v3(self.tD)[:, :, 127])
            for j in range(4):
                stt = 4 * sg + j
                self.mm(by, by[:, sg * 128:(sg + 1) * 128], self.CT[0], self.CT[0][:, stt, :], self.hreb, self.hreb[:, j, :],
                        start=(j == 0), stop=False)
                self.mm(by, by[:, sg * 128:(sg + 1) * 128], self.CT[1], self.CT[1][:, stt, :], self.himb, self.himb[:, j, :],
                        start=False, stop=(j == 3))
        if is_s:
            self.store("sp", self.hre_s, self.hre_s[:], self.hfr, self.hfr[:])
            self.store("sp", self.him_s, self.him_s[:], self.hfi, self.hfi[:])
        elif t == NTILE - 2:
            self.store("sp", self.hre_p, self.hre_p[:], self.carr, self.carr[:])
            self.store("sp", self.him_p, self.him_p[:], self.cari, self.cari[:])
        db = self.dss[:].unsqueeze(2).to_broadcast([128, 4, 128])
        self.tt("dve", self.tA, v3(self.tA), self.uTb, self.uTb[:], self.dss, db, ALU.mult)
        self.tt("dve", self.tB, self.tB[:], self.tA, self.tA[:], by, by[:], ALU.add)
        self.act(self.ygb, self.ygb[:], self.tB, v3(self.tB), AF.Gelu)
        bz = self.nb()
        for co in range(4):
            for ci in range(4):
                self.mm(bz, bz[:, co * 128:(co + 1) * 128], self.wglub, self.wglub[:, ci, co * 128:(co + 1) * 128],
                        self.ygb, self.ygb[:, ci, :], start=(ci == 0), stop=(ci == 3))
        for co in range(4):
            self.act(self.sig, self.sig[:, co, :], bz, bz[:, co * 128:(co + 1) * 128], AF.Sigmoid,
                     bias=self.bglu[:, co:co + 1], extra_reads=[self.bglu])
        self.tt("dve", self.sTb, self.sTb[:], self.ygb, self.ygb[:], self.sig, self.sig[:], ALU.mult)
        if self.debug and t in (0, 1, NTILE - 1):
            self.dbg(f"dbg_aT_{t}", self.aTb, self.aTb[:], [64, 8, 128], BF16)
            self.dbg(f"dbg_sT_{t}", self.sTb, self.sTb[:], [128, 4, 128], BF16)

    def convert_tables(self):
        P = self.P
        self.uT_bf = [P.dram(f"uT_bf{r}", [D, 2048], BF16) for r in range(8)]
        self.v_bf = [P.dram(f"v_bf{r}", [2048, D], BF16) for r in range(8)]
        self.conv_done = 0

    def convert_some(self, n=1):
        P = self.P
        for _ in range(n):
            r = self.conv_done
            if r >= 8:
                return
            self.conv_done += 1
            P.dma("pool", lambda e, r=r: e.dma_start(out=self.uT_bf[r][:], in_=self.uT[:, 2048 * r:2048 * r + 2048]),
                  owner=self.uT_bf[r], reads=[self.uT], writes=[self.uT_bf[r]], nd=64)
            P.dma("pool", lambda e, r=r: e.dma_start(out=self.v_bf[r][:], in_=self.vtab[2048 * r:2048 * r + 2048, :]),
                  owner=self.v_bf[r], reads=[self.vtab], writes=[self.v_bf[r]], nd=128)

    def phase_b_run(self):
        P = self.P
        sb = P.sbuf
        NT = self.NT
        TW = 128 * NT
        self.convert_some(8)
        self.npool = 8 - 2 * NT
        self.bank_i = 0
        self.acc = [[self.banks[self.npool + 2 * i], self.banks[self.npool + 2 * i + 1]] for i in range(NT)]
        self.wqb = sb("wqb", [128, 8, 2048], BF16)
        self.load("pool", self.wqb, self.wqb[:], self.w_q, self.w_q[:].rearrange("(k p) n -> p k n", p=128))
        self.keysb = sb("keysb", [128, 16, 128], BF16)
        self.load("pool", self.keysb, self.keysb[:], self.keysT, self.keysT[:])
        self.wgb = sb("wgb", [128, 8, D], BF16)
        self.load("pool", self.wgb, self.wgb[:], self.w_gate, self.w_gate[:].rearrange("(k p) n -> p k n", p=128))
        self.wppb = sb("wppb", [128, 2, D], BF16)
        self.load("pool", self.wppb, self.wppb[:], self.w_pp, self.w_pp[:].rearrange("(k p) n -> p k n", p=128))
        self.g2 = sb("g2", [128, D], F32)
        self.b2 = sb("b2", [128, D], F32)
        self.load("sp", self.g2, self.g2[:], self.ln2g, self.ln2g[:])
        self.load("sp", self.b2, self.b2[:], self.ln2b, self.ln2b[:])
        self.iotab = sb("iotab", [128, 128], BF16)
        self.load("pool", self.iotab, self.iotab[:], self.iota128, self.iota128[:])
        self.io16 = sb("io16", [128, 16], F32)
        self.load("sp", self.io16, self.io16[:], self.iota16, self.iota16[:])
        EC = self.EC
        self.ut = [sb(f"ut{i}", [128, 8, EC * 128], BF16) for i in range(self.NBUF)]
        self.vt = [sb(f"vt{i}", [128, EC, D], BF16) for i in range(self.NBUF)]
        self.Gall = sb("Gall", [128, TW, 128], BF16)
        self.OH = [sb(f"OH{i}", [128, 1024], F32) for i in range(4)]
        self.x1f = [sb(f"x1f{i}", [128, D], F32) for i in range(1)]
        self.x1b = sb("x1b", [128, D], BF16)
        self.x1T = [sb(f"x1T{i}", [128, 8, TW], BF16) for i in range(2)]
        self.qsB = sb("qsB", [128, D], F32)
        self.candB = sb("candB", [128, 512], F32)
        self.tmp128 = sb("tmp128", [128, 128], F32)
        self.v16 = sb("v16", [128, 16, 16], F32)
        self.i16 = sb("i16", [128, 16, 16], U32)
        self.i16f = sb("i16f", [128, 16, 16], F32)
        self.sc16 = sb("sc16", [128, 8, 16], F32)
        self.ci = sb("ci", [128, 8, 16], U32)
        self.ia = sb("ia", [128, 8, 16], U32)
        self.ib = sb("ib", [128, 8, 16], U32)
        self.iaf = sb("iaf", [128, 8, 16], F32)
        self.ibf = sb("ibf", [128, 8, 16], F32)
        self.e12g = sb("e12g", [128, 3, 128], F32)
        self.gsum = sb("gsum", [128, 8], F32)
        self.egT = [sb(f"egT{i}", [128, 3, TW], BF16) for i in range(2)]
        self.nb2 = [sb(f"nb2{i}", [128, TW], F32) for i in range(2)]
        self.Hg = [sb(f"Hg{i}", [128, TW], BF16) for i in range(self.NB)]
        self.Hm = [sb(f"Hm{i}", [128, TW], BF16) for i in range(self.NB)]
        self.pTb = sb("pTb", [128, 2, 128], BF16)
        self.stats = sb("statsB", [128, 2, 6], F32)
        self.mv = sb("mvB", [128, 2], F32)
        self.rstd = sb("rstdB", [128, 1], F32)
        try:
            print("phase B sbuf bytes remaining:", self.nc.sbuf_bytes_remaining)
        except Exception as ex:
            print("sbuf_bytes_remaining n/a", ex)
        tl = self.tiles
        sts = [tl[i:i + NT] for i in range(0, len(tl), NT)]
        sts.sort(key=lambda g: (len(g) == NT))

        def drain(g):
            for _ in g:
                pass

        def front(si):
            for idx, t in enumerate(sts[si]):
                yield from self.retrieval(t, idx, si % 2)

        def gcon(si):
            for idx, t in enumerate(sts[si]):
                yield from self.gconstruct(t, idx, si % 2)

        def epi(si):
            for idx, t in enumerate(sts[si]):
                yield from self.epilogue_b(t, idx)

        def interleave(ga, gb):
            ga, gb = iter(ga), iter(gb)
            da = db = False
            while not (da and db):
                if not da:
                    try:
                        next(ga)
                    except StopIteration:
                        da = True
                if not db:
                    try:
                        next(gb)
                    except StopIteration:
                        db = True

        drain(front(0))
        drain(gcon(0))
        for si in range(len(sts)):
            bg = front(si + 1) if si + 1 < len(sts) else iter(())
            self.dense(sts[si], si % 2, bg)
            drain(bg)
            if si + 1 < len(sts):
                interleave(epi(si), gcon(si + 1))
            else:
                drain(epi(si))

    def transpose_to(self, srcb, dstT, col0, dstap=None, eng="act"):
        P = self.P
        bk = self.nb()
        bkb = bk[:].bitcast(BF16)
        for kt in range(8):
            P.op("pe", lambda e, kt=kt, bkb=bkb: e.transpose(out=bkb[:, kt * 128:(kt + 1) * 128],
                                                             in_=srcb[:, kt * 128:(kt + 1) * 128], identity=self.identb[:]),
                 reads=[srcb, self.identb], writes=[bk], pe_accum=True)
        dap = dstT[:, :, col0:col0 + 128] if dstap is None else dstap
        self.cp(eng, dstT, dap, bk, bkb.rearrange("p (k n) -> p k n", k=8))

    def retrieval(self, t, idx, par):
        P = self.P
        x1f = self.x1f[0]
        x1T, egT = self.x1T[par], self.egT[par]
        qTB = self.qsB
        qT = self.qsB[:].bitcast(BF16).rearrange("p (i n) -> p i n", i=16)
        candB = self.candB
        cand, cand2 = self.candB[:, 0:256], self.candB[:, 256:512]
        OH = self.OH
        scb = lambda i: OH[i // 8]
        sca = lambda i: OH[i // 8][:, (i % 8) * 128:(i % 8 + 1) * 128]
        tmb = lambda i: OH[2 + i // 8]
        tma = lambda i: OH[2 + i // 8][:, (i % 8) * 128:(i % 8 + 1) * 128]
        self.load("act", x1f, x1f[:], self.X1[t], self.X1[t][:])
        yield
        yield
        self.cp("dve", self.x1b, self.x1b[:], x1f, x1f[:])
        yield
        self.transpose_to(self.x1b, x1T, idx * 128, eng="dve")
        yield
        xc = slice(idx * 128, idx * 128 + 128)
        v3 = lambda b: b[:].rearrange("p (j n) -> p j n", j=4)
        for grp in range(4):
            bq = self.nb()
            for j in range(4):
                i = 4 * grp + j
                for kt in range(8):
                    self.mm(bq, bq[:, j * 128:(j + 1) * 128], self.wqb, self.wqb[:, kt, i * 128:(i + 1) * 128],
                            x1T, x1T[:, kt, xc], start=(kt == 0), stop=(kt == 7))
            yield
            self.cp("dve", qTB, qT[:, 4 * grp:4 * grp + 4, :], bq, v3(bq))
            yield
        for grp in range(4):
            bs = self.nb()
            for j in range(4):
                i = 4 * grp + j
                self.mm(bs, bs[:, j * 128:(j + 1) * 128], qTB, qT[:, i, :], self.keysb, self.keysb[:, i, :])
            yield
            self.cp("dve", OH[grp // 2], OH[grp // 2][:, (grp % 2) * 512:(grp % 2) * 512 + 512], bs, bs[:])
            yield

        def top16(srcB, src, tmpB, tmp, valB, val, idxB, idxv):
            P.op("dve", lambda e: e.max(out=val[:, 0:8], in_=src), reads=[srcB], writes=[valB])
            P.op("dve", lambda e: e.match_replace(out=tmp, in_to_replace=val[:, 0:8], in_values=src, imm_value=-1e30),
                 reads=[srcB, valB], writes=[tmpB])
            P.op("dve", lambda e: e.max(out=val[:, 8:16], in_=tmp), reads=[tmpB], writes=[valB])
            P.op("dve", lambda e: e.max_index(out=idxv[:, 0:8], in_max=val[:, 0:8], in_values=src), reads=[srcB, valB], writes=[idxB])
            P.op("dve", lambda e: e.max_index(out=idxv[:, 8:16], in_max=val[:, 8:16], in_values=tmp), reads=[tmpB, valB], writes=[idxB])

        for i in range(16):
            P.op("dve", lambda e, i=i: e.max(out=self.v16[:, i, 0:8], in_=sca(i)), reads=[scb(i)], writes=[self.v16])
            if i % 4 == 3:
                yield
        for i in range(16):
            P.op("dve", lambda e, i=i: e.match_replace(out=tma(i), in_to_replace=self.v16[:, i, 0:8], in_values=sca(i),
                                                       imm_value=-1e30), reads=[scb(i), self.v16], writes=[tmb(i)])
            if i % 4 == 3:
                yield
        for i in range(16):
            P.op("dve", lambda e, i=i: e.max(out=self.v16[:, i, 8:16], in_=tma(i)), reads=[tmb(i)], writes=[self.v16])
            if i % 4 == 3:
                yield
        for i in range(16):
            P.op("dve", lambda e, i=i: e.max_index(out=self.i16[:, i, 0:8], in_max=self.v16[:, i, 0:8], in_values=sca(i)),
                 reads=[scb(i), self.v16], writes=[self.i16])
            P.op("dve", lambda e, i=i: e.max_index(out=self.i16[:, i, 8:16], in_max=self.v16[:, i, 8:16], in_values=tma(i)),
                 reads=[tmb(i), self.v16], writes=[self.i16])
            if i % 2 == 1:
                yield
        self.cp("dve", self.i16f, self.i16f[:], self.i16, self.i16[:])
        c3 = cand.rearrange("p (a b) -> p a b", a=16)
        for h in range(8):
            self.tt("dve", candB, c3, self.v16, self.v16[:, 2 * h, :].unsqueeze(2).to_broadcast([128, 16, 16]),
                    self.v16, self.v16[:, 2 * h + 1, :].unsqueeze(1).to_broadcast([128, 16, 16]), ALU.add)
            val, idxv = self.sc16[:, h, :], self.ci[:, h, :]
            P.op("dve", lambda e, val=val: e.max(out=val[:, 0:8], in_=cand), reads=[candB], writes=[self.sc16])
            P.op("dve", lambda e, val=val: e.match_replace(out=cand2, in_to_replace=val[:, 0:8], in_values=cand, imm_value=-1e30),
                 reads=[candB, self.sc16], writes=[candB])
            yield
            P.op("dve", lambda e, val=val: e.max(out=val[:, 8:16], in_=cand2), reads=[candB], writes=[self.sc16])
            P.op("dve", lambda e, val=val, idxv=idxv: e.max_index(out=idxv[:, 0:8], in_max=val[:, 0:8], in_values=cand),
                 reads=[candB, self.sc16], writes=[self.ci])
            yield
            P.op("dve", lambda e, val=val, idxv=idxv: e.max_index(out=idxv[:, 8:16], in_max=val[:, 8:16], in_values=cand2),
                 reads=[candB, self.sc16], writes=[self.ci])
            yield
        P.op("dve", lambda e: e.tensor_single_scalar(out=self.ia[:], in_=self.ci[:], scalar=4, op=ALU.logical_shift_right),
             reads=[self.ci], writes=[self.ia])
        P.op("dve", lambda e: e.tensor_single_scalar(out=self.ib[:], in_=self.ci[:], scalar=15, op=ALU.bitwise_and),
             reads=[self.ci], writes=[self.ib])
        self.cp("dve", self.iaf, self.iaf[:], self.ia, self.ia[:])
        self.cp("dve", self.ibf, self.ibf[:], self.ib, self.ib[:])
        yield
        S4 = [128, 4, 16, 16]
        iob = self.io16[:].unsqueeze(1).unsqueeze(1).to_broadcast(S4)
        i4 = self.i16f[:].rearrange("p (h two) a -> p h two a", two=2)
        for which, srcf in enumerate([self.iaf, self.ibf]):
            for hf in range(2):
                hs = slice(4 * hf, 4 * hf + 4)
                ohB = OH[hf]
                oh = OH[hf][:].rearrange("p (h k a) -> p h k a", h=4, k=16)
                self.tt("dve", ohB, oh, srcf, srcf[:, hs, :].unsqueeze(3).to_broadcast(S4), self.io16, iob, ALU.is_equal)
                yield
                self.tt("dve", ohB, oh, ohB, oh, self.i16f, i4[:, hs, which, :].unsqueeze(2).to_broadcast(S4), ALU.mult)
                yield
                P.op("dve", lambda e, which=which, hf=hf, oh=oh: e.tensor_reduce(
                    out=self.e12g[:, which, 64 * hf:64 * hf + 64].rearrange("p (h k) -> p h k", h=4), in_=oh, axis=AX.X, op=ALU.add),
                    reads=[ohB], writes=[self.e12g])
                yield
        g3 = self.e12g[:, 2, :].rearrange("p (h k) -> p h k", h=8)
        self.tt("dve", self.e12g, g3, self.sc16, self.sc16[:], self.sc16, self.sc16[:, :, 0:1].to_broadcast([128, 8, 16]), ALU.subtract)
        for _ in range(10):
            yield
        self.act(self.e12g, g3, self.e12g, g3, AF.Exp)
        yield
        yield
        P.op("dve", lambda e: e.tensor_reduce(out=self.gsum[:], in_=g3, axis=AX.X, op=ALU.add), reads=[self.e12g], writes=[self.gsum])
        P.op("dve", lambda e: e.reciprocal(out=self.gsum[:], in_=self.gsum[:]), reads=[self.gsum], writes=[self.gsum])
        self.tt("dve", self.e12g, g3, self.e12g, g3, self.gsum, self.gsum[:].unsqueeze(2).to_broadcast([128, 8, 16]), ALU.mult)
        yield
        yield
        bt = self.nb()
        for w_ in range(3):
            P.op("pe", lambda e, w_=w_: e.transpose(out=bt[:, w_ * 128:(w_ + 1) * 128], in_=self.e12g[:, w_, :], identity=self.identf[:]),
                 reads=[self.e12g, self.identf], writes=[bt], pe_accum=True)
        yield
        self.cp("dve", egT, egT[:, :, xc], bt, bt[:, 0:384].rearrange("p (w n) -> p w n", w=3))
        yield

    def gconstruct(self, t, idx, par):
        P = self.P
        egT = self.egT[par]
        nb2 = self.nb2[par]
        v3 = lambda b: b[:].rearrange("p (j n) -> p j n", j=4)
        S3 = [128, 16, 128]
        iob3 = self.iotab[:].unsqueeze(1).to_broadcast(S3)
        for st in range(8):
            A1B, BqB = self.OH[st % 2], self.OH[2 + st % 2]
            A1q = A1B[:].bitcast(BF16).rearrange("p (t n) -> p t n", t=16)
            Bq = BqB[:].bitcast(BF16).rearrange("p (t n) -> p t n", t=16)
            qs = slice(idx * 128 + 16 * st, idx * 128 + 16 * st + 16)
            for tq in range(16):
                tok = idx * 128 + 16 * st + tq
                P.op("dve", lambda e, tq=tq, tok=tok, A1q=A1q: e.tensor_scalar(
                    out=A1q[:, tq, :], in0=self.iotab[:], scalar1=egT[:, 0, tok:tok + 1], scalar2=egT[:, 2, tok:tok + 1],
                    op0=ALU.is_equal, op1=ALU.mult), reads=[self.iotab, egT], writes=[A1B])
                beng = "pool" if (tq % 8) < self.BPOOL else "dve"
                P.op(beng, lambda e, tq=tq, tok=tok, Bq=Bq: e.tensor_scalar(
                    out=Bq[:, tq, :], in0=self.iotab[:], scalar1=egT[:, 1, tok:tok + 1], scalar2=None,
                    op0=ALU.is_equal), reads=[self.iotab, egT], writes=[BqB])
            for b4 in range(4):
                bg = self.nb()
                for j in range(4):
                    tq = 4 * b4 + j
                    self.mm(bg, bg[:, j * 128:(j + 1) * 128], BqB, Bq[:, tq, :], A1B, A1q[:, tq, :])
                tok0 = idx * 128 + 16 * st + 4 * b4
                self.cp("act", self.Gall, self.Gall[:, tok0:tok0 + 4, :], bg, v3(bg))
            yield

    def dense(self, tl, par, bg):
        P = self.P
        NT, EC = len(tl), self.EC
        TW = 128 * NT
        x1T = self.x1T[par]
        nchunk = 128 // EC
        NB, NBUF = self.NB, self.NBUF

        def stage1(j):
            c, jj = divmod(j, EC)
            ut, vt = self.ut[c % NBUF], self.vt[c % NBUF]
            if jj == 0:
                r, off = (c * EC) // 16, ((c * EC) % 16) * 128
                self.load("sp", ut, ut[:], self.uT_bf[r], self.uT_bf[r][:, off:off + EC * 128].rearrange("(k p) e -> p k e", p=128))
                self.load("sp", vt, vt[:], self.v_bf[r], self.v_bf[r][off:off + EC * 128, :].rearrange("(j p) d -> p j d", p=128))
            bh = self.nb()
            Hg, Hm = self.Hg[j % NB], self.Hm[j % NB]
            for kt in range(8):
                self.mm(bh, bh[:, 0:TW], ut, ut[:, kt, jj * 128:(jj + 1) * 128], x1T, x1T[:, kt, 0:TW],
                        start=(kt == 0), stop=(kt == 7))
            self.act(Hg, Hg[:, 0:TW], bh, bh[:, 0:TW], AF.Gelu)
            self.tt(self.mask_eng, Hm, Hm[:, 0:TW], Hg, Hg[:, 0:TW], self.Gall, self.Gall[:, 0:TW, j], ALU.mult)

        def stage2(j):
            c, jj = divmod(j, EC)
            vt = self.vt[c % NBUF]
            Hm = self.Hm[j % NB]
            for idx in range(NT):
                for hh in range(2):
                    ab = self.acc[idx][hh]
                    self.mm(ab, ab[:], Hm, Hm[:, idx * 128:(idx + 1) * 128], vt, vt[:, jj, 512 * hh:512 * hh + 512],
                            start=(j == 0), stop=(j == 127))

        SK = NB - 1
        self._bgacc = 0.0
        for j in range(min(SK, 128)):
            stage1(j)
        for j in range(128):
            if j + SK < 128:
                stage1(j + SK)
            stage2(j)
            self._bgacc += self.BGR
            while self._bgacc >= 1.0:
                self._bgacc -= 1.0
                next(bg, None)

    def epilogue_b(self, t, idx):
        P = self.P
        c0 = 128 * t
        sgB, sgt = self.qsB, self.qsB[:]
        x2TB = self.candB
        x2T = self.candB[:].bitcast(BF16).rearrange("p (k n) -> p k n", k=8)
        x = self.x1f[0]
        if self.NT > 1 or True:
            self.load("sp", x, x[:], self.X1[t], self.X1[t][:])
        for hh in range(2):
            ab = self.acc[idx][hh]
            P.op("dve", lambda e, hh=hh, ab=ab: e.scalar_tensor_tensor(
                out=x[:, 512 * hh:512 * hh + 512], in0=x[:, 512 * hh:512 * hh + 512], scalar=ALPHA,
                in1=ab[:], op0=ALU.mult, op1=ALU.add), reads=[x, ab], writes=[x])
        if self.debug and t in (0, 1, NTILE - 1):
            self.dbg(f"dbg_r2_{t}", x, x[:], [128, D])
        yield
        self.layer_norm(x, x, self.g2, self.b2, part=1)
        yield
        self.layer_norm_fin(x, x, self.g2, self.b2)
        yield
        self.cp("act", self.x1b, self.x1b[:], x, x[:])
        yield
        self.transpose_to(self.x1b, x2TB, 0, dstap=x2T)
        yield
        self.load("pool", self.pTb, self.pTb[:], self.pT, self.pT[:, c0:c0 + 128].rearrange("(k p) n -> p k n", p=128))
        for hh in range(2):
            bgt, bpp = self.acc[idx][0], self.acc[idx][1]
            for kt in range(8):
                self.mm(bgt, bgt[:], x2TB, x2T[:, kt, :], self.wgb, self.wgb[:, kt, 512 * hh:512 * hh + 512],
                        start=(kt == 0), stop=(kt == 7))
            for k2 in range(2):
                self.mm(bpp, bpp[:], self.pTb, self.pTb[:, k2, :], self.wppb, self.wppb[:, k2, 512 * hh:512 * hh + 512],
                        start=(k2 == 0), stop=(k2 == 1))
            sl = slice(512 * hh, 512 * hh + 512)
            self.act(sgB, sgt[:, sl], bgt, bgt[:], AF.Sigmoid)
            yield
            self.tt("dve", sgB, sgt[:, sl], sgB, sgt[:, sl], bpp, bpp[:], ALU.mult)
            self.tt("dve", sgB, sgt[:, sl], sgB, sgt[:, sl], x, x[:, sl], ALU.add)
            yield
        self.store("sp", self.y, self.y[c0:c0 + 128, :], sgB, sgt)


def _rope_tables():
    half = 32
    inv = (10000.0 ** (-np.arange(half, dtype=np.float32) / np.float32(half))).astype(np.float32)
    pos = np.concatenate([np.arange(TP, dtype=np.int64), 16384 + (np.arange(TS) % DSEQ)]).astype(np.float32)
    ang = (pos[:, None] * inv[None, :]).astype(np.float32)
    cos = np.cos(ang).astype(np.float32)
    sin = np.sin(ang).astype(np.float32)
    r = np.arange(128)
    dd = r % 64
    j = dd % 32
    sgn = np.where(dd < 32, -1.0, 1.0).astype(np.float32)
    C = np.ascontiguousarray(cos[:, j].T)
    S = np.ascontiguousarray((sin[:, j] * sgn[None, :]).T)
    rows = np.concatenate([np.arange(TP - 128, TP), np.arange(TP, TT)])
    return C, S, np.ascontiguousarray(cos[rows]), np.ascontiguousarray(sin[rows])


def _masks():
    k = np.arange(128)[:, None]
    q = np.arange(128)[None, :]
    own = np.where(q >= k, 0.0, NEG).astype(np.float32)
    prev = np.where(k > q, 0.0, NEG).astype(np.float32)
    same = (k // DSEQ) == (q // DSEQ)
    new = np.where(same & ((k % DSEQ) <= (q % DSEQ)), 0.0, NEG).astype(np.float32)
    qi = np.arange(DSEQ)[None, :]
    past = np.where(k > qi, 0.0, NEG).astype(np.float32)
    t4 = lambda m: np.ascontiguousarray(np.tile(m, (1, 4)))
    return t4(own), t4(prev), t4(new), t4(past)


def _st_layout(a):
    a = np.asarray(a, np.float32)
    rest = a.shape[2:]
    return np.ascontiguousarray(np.moveaxis(a.reshape((16, 128) + rest), 0, 1))


def _prep_shared(inp):
    w = {}
    w_in = np.asarray(inp["w_in"][0], np.float32)
    q = w_in[:, 0:512].reshape(D, 8, 64)
    k = w_in[:, 512:640].reshape(D, 2, 64)
    partner = lambda a: np.concatenate([a[..., 32:], a[..., :32]], axis=-1)
    qt = lambda a: np.concatenate([np.concatenate([a[:, i], a[:, i + 4]], axis=1) for i in range(4)], axis=1)
    cols = [qt(q), qt(partner(q)), k.reshape(D, 128), partner(k).reshape(D, 128), w_in[:, 768:1280],
            w_in[:, 1280:2304], w_in[:, 2304:3328], w_in[:, 640:768]]
    w["w_in"] = np.ascontiguousarray(np.concatenate(cols, axis=1))
    C, S, Ct, St = _rope_tables()
    w["ropeC"], w["ropeS"], w["ropeCt"], w["ropeSt"] = C, S, Ct, St
    w["sinks"] = np.ascontiguousarray(np.tile(np.asarray(inp["attn_sinks"][0], np.float32)[None, :], (128, 1)))
    w["w_ap"] = np.ascontiguousarray(inp["w_attn_proj"][0])
    w["w_sp"] = np.ascontiguousarray(inp["w_ssm_proj"][0])
    w["w_glu"] = np.ascontiguousarray(inp["w_glu"][0])
    w["b_glu"] = np.ascontiguousarray(np.asarray(inp["b_glu"][0], np.float32).reshape(4, 128).T)
    w["w_out"] = np.ascontiguousarray(inp["w_out"][0])
    rep = lambda v: np.ascontiguousarray(np.tile(np.asarray(v, np.float32)[None, :], (128, 1)))
    w["ln1g"], w["ln1b"] = rep(inp["ln1_g"][0]), rep(inp["ln1_b"][0])
    w["ln2g"], w["ln2b"] = rep(inp["ln2_g"][0]), rep(inp["ln2_b"][0])
    w["lam_re"] = _st_layout(inp["ssm_lambda_re"][0])
    w["lam_im"] = _st_layout(inp["ssm_lambda_im"][0])
    w["logdt"] = _st_layout(np.repeat(np.asarray(inp["ssm_log_dt"][0], np.float32)[:, None], 64, axis=1))
    w["Bre"] = _st_layout(inp["ssm_b_re"][0])
    w["Bim"] = _st_layout(inp["ssm_b_im"][0])
    w["Cre"] = _st_layout(np.swapaxes(np.asarray(inp["ssm_c_re"][0]), 1, 2))
    w["Cim"] = _st_layout(np.swapaxes(np.asarray(inp["ssm_c_im"][0]), 1, 2))
    w["dssm"] = np.ascontiguousarray(np.asarray(inp["ssm_d"][0], np.float32).reshape(4, 128).T)
    mo, mp, mn, mpa = _masks()
    w["mask_own"], w["mask_prev"], w["mask_new"], w["mask_past"] = mo, mp, mn, mpa
    w["iota_tau"] = np.ascontiguousarray(np.tile(np.arange(128, dtype=np.float32)[None, :], (128, 1)))
    w["iota128"] = w["iota_tau"]
    w["iota16"] = np.ascontiguousarray(np.tile(np.arange(16, dtype=np.float32)[None, :], (128, 1)))
    w["w_q"] = np.ascontiguousarray(inp["peer_w_q"][0])
    k1 = np.asarray(inp["peer_keys1"][0], np.float32)
    k2 = np.asarray(inp["peer_keys2"][0], np.float32)
    kk = np.stack([k1, k2], axis=1).reshape(16, 128, 128)
    w["keysT"] = np.ascontiguousarray(np.transpose(kk, (2, 0, 1)))
    w["uT"] = np.ascontiguousarray(np.asarray(inp["peer_u"][0], np.float32).T)
    w["vtab"] = np.ascontiguousarray(inp["peer_v"][0])
    w["w_gate"] = np.ascontiguousarray(inp["ple_w_gate"][0])
    w["w_pp"] = np.ascontiguousarray(inp["ple_w_proj"][0])
    return w


def _prep_core(inp, c):
    m = {}
    xs = np.asarray(inp["x_sample"][16 * c:16 * c + 16], np.float32).reshape(TS, D)
    x = np.concatenate([np.asarray(inp["x_prompt"][c], np.float32), xs], axis=0)
    m["x"] = np.ascontiguousarray(x)
    m["xT"] = np.ascontiguousarray(x.T)
    ps = np.asarray(inp["p_sample"][0, 16 * c:16 * c + 16], np.float32).reshape(TS, 256)
    p = np.concatenate([np.asarray(inp["p_prompt"][0, c], np.float32), ps], axis=0)
    m["pT"] = np.ascontiguousarray(p.T)
    wk = np.asarray(inp["state_win_k"][0, 16 * c:16 * c + 16], np.float32)
    wv = np.asarray(inp["state_win_v"][0, 16 * c:16 * c + 16], np.float32)
    m["wkT"] = np.ascontiguousarray(np.transpose(wk, (2, 3, 0, 1)).reshape(128, NSEQ, 128))
    m["wv"] = np.ascontiguousarray(np.transpose(wv, (1, 0, 2, 3)).reshape(128, NSEQ, 128))
    m["wk_raw"] = np.ascontiguousarray(wk.reshape(NSEQ, 128, 128))
    m["wv_raw"] = np.ascontiguousarray(wv.reshape(NSEQ, 128, 128))
    hr = np.asarray(inp["state_ssm_re"][0, 16 * c:16 * c + 16], np.float32)
    hi = np.asarray(inp["state_ssm_im"][0, 16 * c:16 * c + 16], np.float32)
    m["h0re"] = _st_layout(np.moveaxis(hr, 0, 2))
    m["h0im"] = _st_layout(np.moveaxis(hi, 0, 2))
    return m


def _from_st(a):
    rest = a.shape[2:]
    return np.moveaxis(a, 0, 1).reshape((32, 64) + rest)


_CACHE = {}


def kernel(**inputs):
    if "nc" not in _CACHE:
        _CACHE["nc"] = KB(NT=2).build()
    nc = _CACHE["nc"]
    shared = _prep_shared(inputs)
    in_maps = []
    for c in range(NCORES):
        m = dict(shared)
        m.update(_prep_core(inputs, c))
        in_maps.append(m)
    res = run_bass_kernel_spmd(nc, in_maps, core_ids=list(range(NCORES)))
    R = res.results
    y_p = np.stack([R[c]["y"][:TP] for c in range(NCORES)], 0).astype(np.float32)
    y_s = np.concatenate([R[c]["y"][TP:].reshape(NSEQ, DSEQ, D) for c in range(NCORES)], 0).astype(np.float32)
    nk_p = np.stack([R[c]["nk_p"].reshape(128, 2, 64) for c in range(NCORES)], 0)[None].astype(np.float32)
    nv_p = np.stack([R[c]["nv_p"].reshape(128, 2, 64) for c in range(NCORES)], 0)[None].astype(np.float32)
    hre_p = np.stack([_from_st(R[c]["hre_p"]) for c in range(NCORES)], 0)[None].astype(np.float32)
    him_p = np.stack([_from_st(R[c]["him_p"]) for c in range(NCORES)], 0)[None].astype(np.float32)
    nk_s = np.concatenate([R[c]["nk_s"].reshape(NSEQ, 128, 2, 64) for c in range(NCORES)], 0)[None].astype(np.float32)
    nv_s = np.concatenate([R[c]["nv_s"].reshape(NSEQ, 128, 2, 64) for c in range(NCORES)], 0)[None].astype(np.float32)
    hre_s = np.concatenate([np.moveaxis(_from_st(R[c]["hre_s"]), 2, 0) for c in range(NCORES)], 0)[None].astype(np.float32)
    him_s = np.concatenate([np.moveaxis(_from_st(R[c]["him_s"]), 2, 0) for c in range(NCORES)], 0)[None].astype(np.float32)
    return (y_p, y_s, nk_p, nv_p, hre_p, him_p, nk_s, nv_s, hre_s, him_s)
```
